# Optimizing a Trainium2 kernel written in Bass

```python
import math
import jax, jax.numpy as jnp
from jax import lax
import numpy as np

D_MODEL = 1024
BATCH = 1
SEQ = 16384
DEPTH = 1

ROPE_THETA = 10000.0
NORM_EPS = 1e-6
Q_BLOCK = 128

DIFF_HEADS = 4
DIFF_HEAD_DIM = 64
DIFF_V_DIM = 2 * DIFF_HEAD_DIM
DIFF_WIDTH = DIFF_HEADS * DIFF_V_DIM
DIFF_QK_COLS = DIFF_HEADS * 2 * DIFF_HEAD_DIM

MLA_HEADS = 4
MLA_NOPE = 128
MLA_ROPE = 64
MLA_V = 128
MLA_Q_RANK = 384
MLA_KV_RANK = 256
MLA_WIDTH = MLA_HEADS * MLA_V

SPLIT_SIZES = (DIFF_QK_COLS, DIFF_QK_COLS, DIFF_WIDTH, DIFF_WIDTH,
               MLA_Q_RANK, MLA_KV_RANK, MLA_ROPE, MLA_WIDTH,
               D_MODEL, D_MODEL)
IN_COLS = sum(SPLIT_SIZES)

kernel_name = "hybrid_diffattn_mla_gated_block"


def rms_norm(x, gain):
    xf = x.astype(jnp.float32)
    y = xf * lax.rsqrt(jnp.mean(xf * xf, axis=-1, keepdims=True) + NORM_EPS)
    return (y * gain.astype(jnp.float32)).astype(x.dtype)


def rope_tables(positions, dim):
    inv_freq = ROPE_THETA ** (-jnp.arange(0, dim, 2, dtype=jnp.float32) / dim)
    ang = positions.astype(jnp.float32)[..., None] * inv_freq
    return jnp.cos(ang), jnp.sin(ang)


def apply_rope(x, cos, sin):
    b, s, half = cos.shape
    shape = (b,) + (1,) * (x.ndim - 3) + (s, half)
    c = cos.reshape(shape).astype(x.dtype)
    sn = sin.reshape(shape).astype(x.dtype)
    x1, x2 = x[..., :half], x[..., half:]
    return jnp.concatenate([x1 * c - x2 * sn, x2 * c + x1 * sn], axis=-1)


def causal_multimap_attention(q, k, v, map_weights, scale):
    b, h, m, s, dk = q.shape
    dv = v.shape[-1]
    nb = s // Q_BLOCK
    qb = jnp.moveaxis(q.reshape(b, h, m, nb, Q_BLOCK, dk), 3, 0)
    kpos = jnp.arange(s)
    w = map_weights.astype(jnp.float32)

    def one_block(args):
        q_blk, i = args
        qpos = i * Q_BLOCK + jnp.arange(Q_BLOCK)
        sc = jnp.einsum('bhmqd,bhmkd->bhmqk', q_blk, k).astype(jnp.float32) * scale
        sc = jnp.where(kpos[None, :] <= qpos[:, None], sc, -jnp.inf)
        p = jax.nn.softmax(sc, axis=-1)
        p = jnp.einsum('m,bhmqk->bhqk', w, p)
        return jnp.einsum('bhqk,bhkd->bhqd', p.astype(v.dtype), v)

    out = lax.map(one_block, (qb, jnp.arange(nb)))
    return jnp.moveaxis(out, 0, 2).reshape(b, h, s, dv)


def setup_inputs(seed: int = 0) -> dict:
    key = jax.random.key(seed)
    ks = jax.random.split(key, 20)
    L = DEPTH
    nrm = lambda k, shape, fan: jax.random.normal(k, shape, jnp.float32) * fan ** -0.5
    gain = lambda k, n: 1.0 + 0.05 * jax.random.normal(k, (L, n), jnp.float32)
    x = jax.random.normal(ks[0], (BATCH, SEQ, D_MODEL), jnp.float32)
    positions = jnp.broadcast_to(jnp.arange(SEQ, dtype=jnp.int32), (BATCH, SEQ))
    return {
        "x": x,
        "positions": positions,
        "norm_in": gain(ks[1], D_MODEL),
        "w_in": nrm(ks[2], (L, D_MODEL, IN_COLS), D_MODEL),
        "diff_lambda_q1": 0.1 * jax.random.normal(ks[3], (L, DIFF_HEAD_DIM), jnp.float32),
        "diff_lambda_k1": 0.1 * jax.random.normal(ks[4], (L, DIFF_HEAD_DIM), jnp.float32),
        "diff_lambda_q2": 0.1 * jax.random.normal(ks[5], (L, DIFF_HEAD_DIM), jnp.float32),
        "diff_lambda_k2": 0.1 * jax.random.normal(ks[6], (L, DIFF_HEAD_DIM), jnp.float32),
        "diff_subln": gain(ks[7], DIFF_V_DIM),
        "mla_q_norm": gain(ks[8], MLA_Q_RANK),
        "w_uq": nrm(ks[9], (L, MLA_Q_RANK, MLA_HEADS * (MLA_NOPE + MLA_ROPE)), MLA_Q_RANK),
        "mla_kv_norm": gain(ks[10], MLA_KV_RANK),
        "w_ukv": nrm(ks[11], (L, MLA_KV_RANK, MLA_HEADS * (MLA_NOPE + MLA_V)), MLA_KV_RANK),
        "w_proj_diff": nrm(ks[12], (L, DIFF_WIDTH, D_MODEL), DIFF_WIDTH),
        "w_proj_mla": nrm(ks[13], (L, MLA_WIDTH, D_MODEL), MLA_WIDTH),
        "w_out": nrm(ks[14], (L, D_MODEL, D_MODEL), D_MODEL),
        "norm_final": 1.0 + 0.05 * jax.random.normal(ks[15], (D_MODEL,), jnp.float32),
    }


def reference(x, positions, norm_in, w_in, diff_lambda_q1, diff_lambda_k1,
              diff_lambda_q2, diff_lambda_k2, diff_subln, mla_q_norm, w_uq,
              mla_kv_norm, w_ukv, w_proj_diff, w_proj_mla, w_out, norm_final):
    b, s, _ = x.shape
    cos_d, sin_d = rope_tables(positions, DIFF_HEAD_DIM)
    cos_m, sin_m = rope_tables(positions, MLA_ROPE)
    split_idx = list(np.cumsum(SPLIT_SIZES)[:-1])

    for layer in range(DEPTH):
        lambda_init = 0.8 - 0.6 * math.exp(-0.3 * layer)
        h = rms_norm(x, norm_in[layer])
        proj = h @ w_in[layer]
        (dq, dk, dv, dgate, cq, ckv, kr, mgate, g_diff, g_mla) = jnp.split(proj, split_idx, axis=-1)

        dq = dq.reshape(b, s, DIFF_HEADS, 2, DIFF_HEAD_DIM).transpose(0, 2, 3, 1, 4)
        dk = dk.reshape(b, s, DIFF_HEADS, 2, DIFF_HEAD_DIM).transpose(0, 2, 3, 1, 4)
        dv = dv.reshape(b, s, DIFF_HEADS, DIFF_V_DIM).transpose(0, 2, 1, 3)
        dq = apply_rope(dq, cos_d, sin_d)
        dk = apply_rope(dk, cos_d, sin_d)
        lam = (jnp.exp(jnp.sum(diff_lambda_q1[layer] * diff_lambda_k1[layer]))
               - jnp.exp(jnp.sum(diff_lambda_q2[layer] * diff_lambda_k2[layer]))
               + lambda_init)
        weights = jnp.stack([jnp.ones_like(lam), -lam])
        o_diff = causal_multimap_attention(dq, dk, dv, weights, DIFF_HEAD_DIM ** -0.5)
        o_diff = rms_norm(o_diff, diff_subln[layer]) * (1.0 - lambda_init)
        o_diff = o_diff.transpose(0, 2, 1, 3).reshape(b, s, DIFF_WIDTH)
        o_diff = o_diff * jax.nn.silu(dgate)

        cq = rms_norm(cq, mla_q_norm[layer])
        q = (cq @ w_uq[layer]).reshape(b, s, MLA_HEADS, MLA_NOPE + MLA_ROPE).transpose(0, 2, 1, 3)
        q_nope, q_rope = q[..., :MLA_NOPE], apply_rope(q[..., MLA_NOPE:], cos_m, sin_m)
        ckv = rms_norm(ckv, mla_kv_norm[layer])
        kv = (ckv @ w_ukv[layer]).reshape(b, s, MLA_HEADS, MLA_NOPE + MLA_V).transpose(0, 2, 1, 3)
        k_nope, mv = kv[..., :MLA_NOPE], kv[..., MLA_NOPE:]
        k_rope = apply_rope(kr[:, None], cos_m, sin_m)
        mq = jnp.concatenate([q_nope, q_rope], axis=-1)[:, :, None]
        mk = jnp.concatenate([k_nope, jnp.broadcast_to(k_rope, k_nope.shape[:-1] + (MLA_ROPE,))], axis=-1)[:, :, None]
        o_mla = causal_multimap_attention(mq, mk, mv, jnp.ones((1,), jnp.float32),
                                          (MLA_NOPE + MLA_ROPE) ** -0.5)
        o_mla = o_mla.transpose(0, 2, 1, 3).reshape(b, s, MLA_WIDTH)
        o_mla = o_mla * jax.nn.silu(mgate)

        merged = (jax.nn.sigmoid(g_diff) * (o_diff @ w_proj_diff[layer])
                  + jax.nn.sigmoid(g_mla) * (o_mla @ w_proj_mla[layer]))
        x = x + merged @ w_out[layer]

    return rms_norm(x, norm_final)
```

```python
import contextlib
import math
import os
import numpy as np
import concourse.bass as bass
import concourse.mybir as mybir
from concourse.bass_utils import run_bass_kernel_spmd

F32 = mybir.dt.float32
BF16 = mybir.dt.bfloat16
I32 = mybir.dt.int32
AF = mybir.ActivationFunctionType
ALU = mybir.AluOpType

PE, ACT, DVE, POOL, SP = "tensor", "scalar", "vector", "gpsimd", "sync"
ENGS = (PE, ACT, DVE, POOL, SP)

NCORES = 8
SEQ = 16384
DM = 1024
NT = SEQ // NCORES
TPC = NT // 128
EPS = 1e-6
LAMBDA_INIT = 0.8 - 0.6 * math.exp(-0.3 * 0)
NEG = -30000.0

KD = 0
KN = KD + 4 * NT
KR = KN + 4 * NT
VD = KR + NT
VM = VD + 4 * TPC * 129
KVCOLS = VM + 4 * TPC * 129

NFM = 14 * 128
USE_CC = False
NSLOT = 1 if USE_CC else NCORES


class Tok:
    __slots__ = ("sem", "val")

    def __init__(self, sem, val=None):
        self.sem = sem
        self.val = val


class Buf:
    __slots__ = ("name", "w", "r")

    def __init__(self, name=""):
        self.name = name
        self.w = None
        self.r = []


class Prog:
    def __init__(self, nc, es):
        self.nc = nc
        self.es = es
        self.q = {e: [] for e in ENGS}
        self.esem = {e: es.enter_context(nc.semaphore("s_" + e)) for e in ENGS if e != SP}
        self.ecnt = {e: 0 for e in ENGS}
        self.pending = {e: [] for e in ENGS}
        self.seen = {e: {} for e in ENGS}
        self.dsem = {}
        self.dcnt = {}
        self.always = []

    def _waits(self, eng, reads, writes, exclude=None):
        toks = []
        for b in reads:
            if b.w is not None:
                toks.append(b.w)
        for b in writes:
            if b.w is not None:
                toks.append(b.w)
            toks.extend(b.r)
        need = {}
        own = self.esem.get(eng)
        for t in toks:
            if t.sem is exclude:
                continue
            if t.val is None and t.sem is own:
                continue
            assert t.val is not None, "wait on unresolved token (missing signal)"
            k = id(t.sem)
            if k not in need or need[k][1] < t.val:
                need[k] = (t.sem, t.val)
        out = []
        seen = self.seen[eng]
        for k, (sem, val) in need.items():
            if seen.get(k, 0) >= val:
                continue
            seen[k] = val
            out.append((sem, val))
        return out

    def op(self, eng, fn, reads=(), writes=(), signal=True):
        reads = list(reads) + self.always
        waits = self._waits(eng, reads, writes)
        if signal:
            self.ecnt[eng] += 1
            val = self.ecnt[eng]
            sem = self.esem[eng]
            for t in self.pending[eng]:
                t.val = val
            self.pending[eng] = []
            tok = Tok(sem, val)

            def run(e, fn=fn, waits=waits, sem=sem):
                for (s, v) in waits:
                    e.wait_ge(s, v)
                fn(e).then_inc(sem, 1)
        else:
            tok = Tok(self.esem[eng], None)
            self.pending[eng].append(tok)

            def run(e, fn=fn, waits=waits):
                for (s, v) in waits:
                    e.wait_ge(s, v)
                fn(e)
        self.q[eng].append(run)
        for b in writes:
            b.w = tok
            b.r = []
        for b in reads:
            b.r.append(tok)
        return tok

    def dma(self, queue, out, in_, reads=(), writes=(), key=None):
        if key is None:
            key = writes[0].name
        if key not in self.dsem:
            self.dsem[key] = self.es.enter_context(self.nc.semaphore("d_" + key))
            self.dcnt[key] = 0
        waits = self._waits(queue, reads, writes, exclude=self.dsem[key])
        self.dcnt[key] += 16
        sem = self.dsem[key]
        tok = Tok(sem, self.dcnt[key])

        def run(e, waits=waits, sem=sem, out=out, in_=in_):
            for (s, v) in waits:
                e.wait_ge(s, v)
            e.dma_start(out=out, in_=in_).then_inc(sem, 16)
        self.q[queue].append(run)
        for b in writes:
            b.w = tok
            b.r = []
        for b in reads:
            b.r.append(tok)
        return tok

    def raw(self, eng, fn):
        self.q[eng].append(fn)

    def all_tokens(self):
        toks = [Tok(self.esem[e], self.ecnt[e]) for e in self.esem if self.ecnt[e] > 0]
        toks += [Tok(self.dsem[k], self.dcnt[k]) for k in self.dsem]
        return toks

    def barrier(self, engines=ENGS):
        toks = self.all_tokens()
        for eng in engines:
            lst = []
            seen = self.seen[eng]
            for t in toks:
                k = id(t.sem)
                if seen.get(k, 0) >= t.val:
                    continue
                seen[k] = t.val
                lst.append((t.sem, t.val))

            def run(e, lst=lst):
                for (s, v) in lst:
                    e.wait_ge(s, v)
            self.q[eng].append(run)

    def emit(self):
        with self.nc.Block() as block:
            @block.tensor
            def _(e):
                for f in self.q[PE]:
                    f(e)

            @block.scalar
            def _(e):
                for f in self.q[ACT]:
                    f(e)

            @block.vector
            def _(e):
                for f in self.q[DVE]:
                    f(e)

            @block.gpsimd
            def _(e):
                for f in self.q[POOL]:
                    f(e)

            @block.sync
            def _(e):
                for f in self.q[SP]:
                    f(e)


class Arena:
    def __init__(self, ap, nbytes):
        self.ap = ap
        self.nbytes = nbytes
        self.cur = 0

    def at(self, off, n, dt):
        size = 2 if dt == BF16 else 4
        nb = (n * size + 31) // 32 * 32
        assert off % 4 == 0 and off + nb <= self.nbytes, ("arena overflow", off, nb, self.nbytes)
        v = self.ap[:, off // 4:(off + nb) // 4]
        if dt != F32:
            v = v.bitcast(dt)
        return v[:, 0:n], off + nb

    def alloc(self, n, dt):
        v, self.cur = self.at(self.cur, n, dt)
        return v


def build_program(debug=False, stop=0):
    nc = bass.Bass("TRN2", target_bir_lowering=False)

    def din(name, shape, dt=F32):
        return nc.dram_tensor(name, list(shape), dt, kind="ExternalInput").ap()

    xT = din("xT", [NSLOT * 4, 128, 8 * 512])
    xtok = din("xtok", [NT, DM])
    pos = din("pos", [1, NSLOT * NT], I32)
    wfm = din("wfm", [DM, NFM])
    wdv = din("wdv", [DM, 512])
    wuq = din("wuq", [384, 768])
    perm_d = din("perm", [128, 128])
    wukv = din("wukv", [256, 1024])
    wg3 = din("wg3", [DM, 3072])
    wpd = din("wpd", [512, DM])
    wpm = din("wpm", [512, DM])
    wout = din("wout", [DM, DM])
    cst = din("cst", [128, 16])
    lamv = din("lamv", [1, 256])
    subln = din("subln", [1, 128])
    nfin = din("nfin", [1, DM])
    ident_d = din("ident", [128, 128])
    mask_d = din("mask", [128, 1024])
    y = nc.dram_tensor("y", [NT, DM], F32, kind="ExternalOutput").ap()
    kv_in = nc.dram_tensor("kv_in", [128, KVCOLS], BF16, kind="Internal").ap()
    kv_all = nc.dram_tensor("kv_all", [128 * NCORES, KVCOLS], BF16, kind="Internal").ap()
    hT_d = nc.dram_tensor("hT_d", [128, 8 * NT], BF16, kind="Internal").ap()
    dbg = {}
    if debug:
        dbg["kv"] = nc.dram_tensor("dbg_kv", [128, KVCOLS], BF16, kind="ExternalOutput").ap()
        dbg["qt"] = nc.dram_tensor("dbg_qt", [128, 10 * NT], BF16, kind="ExternalOutput").ap()
        dbg["ot"] = nc.dram_tensor("dbg_ot", [128, 8 * NT], BF16, kind="ExternalOutput").ap()

    with contextlib.ExitStack() as es:
        P = Prog(nc, es)
        ARENA_BYTES = 211968
        arena_t = es.enter_context(nc.sbuf_tensor("arena", [128, ARENA_BYTES // 4], F32))
        A = Arena(arena_t[:], ARENA_BYTES)
        ps = es.enter_context(nc.psum_tensor("ps", [128, 4096], F32))
        cc_sem = es.enter_context(nc.semaphore("cc"))

        def bank(b):
            return ps[:, b * 512:(b + 1) * 512]

        psb = [Buf("psb%d" % i) for i in range(8)]
        rr = [0]

        def next_bank():
            b = rr[0] % 8
            rr[0] += 1
            return b

        ident = A.alloc(128, BF16)
        ones = A.alloc(128, BF16)
        maskb = A.alloc(1024, BF16).rearrange("p (r q) -> p r q", r=8)
        cstt = A.alloc(16, F32)
        lamt = A.alloc(256, F32)
        lamw = A.alloc(8, F32)
        sublnb = A.alloc(128, F32)
        nfb = A.alloc(DM, F32)
        half = A.alloc(8, F32)
        junk64 = A.alloc(64, F32)
        Bc = Buf("consts")

        def pbcast(ap):
            v = ap.partition_broadcast(128)
            if len(v.shape) == 3:
                v = v.rearrange("p o n -> p (o n)")
            return v

        P.dma(POOL, ident, ident_d, writes=[Bc], key="c0")
        P.dma(POOL, maskb.rearrange("p r q -> p (r q)"), mask_d, writes=[Bc], key="c0")
        P.dma(POOL, cstt, cst, writes=[Bc], key="c0")
        P.dma(POOL, lamt, pbcast(lamv), writes=[Bc], key="c0")
        P.dma(POOL, sublnb, pbcast(subln), writes=[Bc], key="c0")
        P.dma(POOL, nfb, pbcast(nfin), writes=[Bc], key="c0")
        P.op(DVE, lambda e: e.memset(ones, 1.0), writes=[Bc])
        P.op(DVE, lambda e: e.memset(half[:, 0:1], EPS), writes=[Bc])
        P.op(DVE, lambda e: e.memset(half[:, 1:2], 0.0), writes=[Bc])
        P.op(DVE, lambda e: e.memset(half[:, 2:3], float(np.pi / 2)), writes=[Bc])
        P.op(DVE, lambda e: e.memset(half[:, 3:4], -0.5), writes=[Bc])
        epsb = half[:, 0:1]
        pio2 = half[:, 2:3]
        mhalf = half[:, 3:4]
        P.op(DVE, lambda e: e.scalar_tensor_tensor(out=junk64, in0=lamt[:, 0:64], scalar=1.0, in1=lamt[:, 64:128],
                                                   op0=ALU.mult, op1=ALU.mult, accum_out=lamw[:, 0:1]),
             reads=[Bc], writes=[Bc])
        P.op(DVE, lambda e: e.scalar_tensor_tensor(out=junk64, in0=lamt[:, 128:192], scalar=1.0, in1=lamt[:, 192:256],
                                                   op0=ALU.mult, op1=ALU.mult, accum_out=lamw[:, 1:2]),
             reads=[Bc], writes=[Bc])
        P.op(ACT, lambda e: e.activation(out=lamw[:, 2:4], in_=lamw[:, 0:2], func=AF.Exp), reads=[Bc], writes=[Bc])
        P.op(DVE, lambda e: e.scalar_tensor_tensor(out=lamw[:, 4:5], in0=lamw[:, 3:4], scalar=-LAMBDA_INIT, in1=lamw[:, 2:3],
                                                   op0=ALU.add, op1=ALU.subtract), reads=[Bc], writes=[Bc])
        neglam = lamw[:, 4:5]
        Blam = Bc
        P.op(DVE, lambda e: e.tensor_scalar(out=sublnb, in0=sublnb, scalar1=1.0 - LAMBDA_INIT, scalar2=None, op0=ALU.mult),
             reads=[Bc], writes=[Bc])
        P.op(DVE, lambda e: e.tensor_scalar(out=cstt[:, 15:16], in0=cstt[:, 15:16], scalar1=1.0 - LAMBDA_INIT, scalar2=None, op0=ALU.mult),
             reads=[Bc], writes=[Bc])
        P.always = [Bc]
        g_in = cstt[:, 0:8]
        g_q = cstt[:, 8:11]
        g_kv = cstt[:, 11:13]
        invf = cstt[:, 13:14]
        sgn = cstt[:, 14:15]
        sublnc = cstt[:, 15:16]
        CONST_END = A.cur

        QTd = A.alloc(4 * NT, BF16).rearrange("p (h t) -> p h t", h=4)
        QTn = A.alloc(4 * NT, BF16).rearrange("p (h t) -> p h t", h=4)
        QTr = A.alloc(2 * NT, BF16).rearrange("p (h t) -> p h t", h=2)
        BQ = Buf("QT")
        X0 = A.cur

        Wfm = A.alloc(8 * NFM, BF16).rearrange("p (c n) -> p c n", c=8)
        Wdv = A.alloc(8 * 512, BF16).rearrange("p (c n) -> p c n", c=8)
        Wuq = A.alloc(3 * 768, BF16).rearrange("p (c n) -> p c n", c=3)
        Wukv = A.alloc(2 * 1024, BF16).rearrange("p (c n) -> p c n", c=2)
        permb = A.alloc(128, BF16)
        xTg = A.alloc(8 * 512, F32).rearrange("p (c n) -> p c n", c=8)
        sqg = A.alloc(8 * 512, BF16).rearrange("p (c n) -> p c n", c=8)
        sq2 = A.alloc(5 * 512, BF16).rearrange("p (c n) -> p c n", c=5)
        rstd2 = [A.alloc(512, F32) for _ in range(2)]
        hTg2 = [A.alloc(8 * 512, BF16).rearrange("p (c n) -> p c n", c=8) for _ in range(2)]
        cos2 = [A.alloc(512, F32) for _ in range(2)]
        sin2 = [A.alloc(512, F32) for _ in range(2)]
        tA = A.alloc(512, F32)
        tB = A.alloc(512, F32)
        posi = A.alloc(512, I32)
        mA2 = [A.alloc(512, F32) for _ in range(2)]
        mB2 = [A.alloc(512, F32) for _ in range(2)]
        rawbf = [A.alloc(512, BF16) for _ in range(2)]
        cqraw = A.alloc(3 * 512, F32).rearrange("p (c n) -> p c n", c=3)
        cqn = A.alloc(3 * 512, BF16).rearrange("p (c n) -> p c n", c=3)
        rstdq = A.alloc(512, F32)
        ckvraw = A.alloc(2 * 512, F32).rearrange("p (c n) -> p c n", c=2)
        ckvn = A.alloc(2 * 512, BF16).rearrange("p (c n) -> p c n", c=2)
        rstdkv = A.alloc(512, F32)
        kTd = A.alloc(4 * 512, BF16).rearrange("p (h n) -> p h n", h=4)
        kTn = A.alloc(4 * 512, BF16).rearrange("p (h n) -> p h n", h=4)
        kTr = A.alloc(512, BF16)
        Vd = A.alloc(4 * 4 * 129, BF16).rearrange("p (t h j) -> p t h j", t=4, h=4)
        Vm = A.alloc(4 * 4 * 129, BF16).rearrange("p (t h j) -> p t h j", t=4, h=4)
        P1_END = A.cur

        BW = Buf("w1")
        wfm_v = wfm.rearrange("(c p) n -> p c n", p=128)
        P.dma(POOL, permb, perm_d, writes=[BW], key="w1")
        for c in range(8):
            P.dma(POOL, Wfm[:, c, :], wfm_v[:, c, :], writes=[BW], key="w1")
        P.dma(POOL, Wdv, wdv.rearrange("(c p) n -> p c n", p=128), writes=[BW], key="w1")
        P.dma(POOL, Wuq, wuq.rearrange("(c p) n -> p c n", p=128), writes=[BW], key="w1")
        P.dma(POOL, Wukv, wukv.rearrange("(c p) n -> p c n", p=128), writes=[BW], key="w1")

        BxT, Bsq, Bsq2, BtA, BtB, Bpos = [Buf(n) for n in "xT sq sq2 tA tB posi".split()]
        Brstd2 = [Buf("rstd0"), Buf("rstd1")]
        BhT2 = [Buf("hT0"), Buf("hT1")]
        Btab2 = [Buf("tab0"), Buf("tab1")]
        BmA2 = [Buf("mA0"), Buf("mA1")]
        BmB2 = [Buf("mB0"), Buf("mB1")]
        Braw = [Buf("raw0"), Buf("raw1")]
        Bcq, Bcqn, Brq, Bckv, Bckvn, Brkv = [Buf(n) for n in "cq cqn rq ckv ckvn rkv".split()]
        BkTd, BkTn, BkTr, BVd, BVm = [Buf(n) for n in "kTd kTn kTr Vd Vm".split()]
        Bkvin = Buf("kvin")
        BhTd = Buf("hTd")
        for tt in range(4):
            P.op(POOL, lambda e, tt=tt: e.memset(Vd[:, tt, :, 128:129], 1.0), writes=[BVd])
            P.op(POOL, lambda e, tt=tt: e.memset(Vm[:, tt, :, 128:129], 1.0), writes=[BVm])

        kvin_k = lambda dst, base: dst[:, base:base + 4 * NT].rearrange("p (h t) -> p h t", h=4)
        kvin_v = lambda dst, base: dst[:, base:base + 4 * TPC * 129].rearrange("p (h m j) -> p m h j", h=4, m=TPC)

        def rstd_from_psum(b, nfeat, rdst, Brd):
            if "s" in os.environ.get("XP", ""):
                P.op(ACT, lambda e: e.activation(out=rdst, in_=bank(b), func=AF.Sqrt, scale=1.0 / nfeat, bias=epsb),
                     reads=[psb[b]], writes=[Brd])
                P.op(DVE, lambda e: e.reciprocal(out=rdst, in_=rdst), reads=[Brd], writes=[Brd])
                return
            P.op(ACT, lambda e: e.activation(out=rdst, in_=bank(b), func=AF.Ln, scale=1.0 / nfeat, bias=epsb),
                 reads=[psb[b]], writes=[Brd])
            P.op(ACT, lambda e: e.activation(out=rdst, in_=rdst, func=AF.Exp, scale=-0.5), reads=[Brd], writes=[Brd])

        def proj_fm(W, ncs, col0, rhs, Brhs, ncols=128, BWx=None):
            BWx = BWx or BW
            b = next_bank()
            for c in range(ncs):
                P.op(PE, lambda e, c=c: e.matmul(bank(b)[0:ncols, :], lhsT=W[:, c, col0:col0 + ncols], rhs=rhs[:, c, :],
                                                 start=(c == 0), stop=(c == ncs - 1)),
                     reads=[BWx, Brhs], writes=[psb[b]] if c in (0, ncs - 1) else [], signal=(c == ncs - 1))
            return b

        rawctr = [0]

        def rope_a(W, ncs, col0, rhs, Brhs):
            bA = proj_fm(W, ncs, col0, rhs, Brhs)
            k = rawctr[0] % 2
            rawctr[0] += 1
            P.op(ACT, lambda e: e.activation(out=rawbf[k], in_=bank(bA), func=AF.Copy), reads=[psb[bA]], writes=[Braw[k]])
            return (bA, k)

        def rope_b(st, dst, Bdst, nb):
            bA, k = st
            bB = next_bank()
            P.op(PE, lambda e: e.matmul(bank(bB), lhsT=permb, rhs=rawbf[k], start=True, stop=True), reads=[Braw[k], BW], writes=[psb[bB]])
            mA, mB, BmA, BmB = mA2[k], mB2[k], BmA2[k], BmB2[k]
            P.op(DVE, lambda e: e.tensor_tensor(out=mA, in0=bank(bA), in1=cos2[nb], op=ALU.mult), reads=[psb[bA], Btab2[nb], Braw[k]], writes=[BmA])
            P.op(DVE, lambda e: e.tensor_tensor(out=mB, in0=bank(bB), in1=sin2[nb], op=ALU.mult), reads=[psb[bB], Btab2[nb]], writes=[BmB])
            P.op(POOL, lambda e: e.tensor_tensor(out=dst, in0=mA, in1=mB, op=ALU.add), reads=[BmA, BmB], writes=[Bdst])

        def rope_tiles(specs, nb):
            prev = None
            for sp_ in specs:
                st = rope_a(*sp_[0:5])
                if prev is not None:
                    rope_b(prev[0], prev[1], prev[2], nb)
                prev = (st, sp_[5], sp_[6])
            rope_b(prev[0], prev[1], prev[2], nb)

        groups = [(slot, G) for slot in range(NSLOT) for G in range(4)]
        if os.environ.get("NG"):
            groups = groups[:int(os.environ["NG"])]

        def load_x(n):
            slot, G = groups[n]
            P.dma(SP, xTg.rearrange("p c t -> p (c t)"), xT[slot * 4 + G], writes=[BxT])

        def norm(n, part=0):
            slot, G = groups[n]
            nb = n % 2
            full = slot == 0
            hTg = hTg2[nb]
            tg = slice(G * 512, (G + 1) * 512)
            if part in (0, 1):
                P.op(ACT, lambda e: e.activation(out=sqg, in_=xTg, func=AF.Square), reads=[BxT], writes=[Bsq])
                if part == 1:
                    return
            if part == 3:
                return norm_tables(n)
            b = next_bank()
            for c in range(8):
                P.op(PE, lambda e, c=c, b=b: e.matmul(bank(b), lhsT=ones, rhs=sqg[:, c, :], start=(c == 0), stop=(c == 7)),
                     reads=[Bsq], writes=[psb[b]] if c in (0, 7) else [], signal=(c == 7))
            rstd_from_psum(b, 1024.0, rstd2[nb], Brstd2[nb])
            for c in range(8):
                P.op(DVE, lambda e, c=c: e.scalar_tensor_tensor(out=hTg[:, c, :], in0=xTg[:, c, :], scalar=g_in[:, c:c + 1],
                                                                 in1=rstd2[nb], op0=ALU.mult, op1=ALU.mult),
                     reads=[BxT, Brstd2[nb]], writes=[BhT2[nb]])
            if n + 1 < len(groups):
                load_x(n + 1)
            if full:
                P.dma(SP, hT_d.rearrange("p (c t) -> p c t", c=8)[:, :, tg], hTg, reads=[BhT2[nb]], writes=[BhTd], key="hTd%d" % nb)
            if part == 2:
                return
            norm_tables(n)

        def load_pos(n):
            slot, G = groups[n]
            P.dma(SP, posi, pbcast(pos[:, slot * NT + G * 512:slot * NT + (G + 1) * 512]), writes=[Bpos], key="tab")

        def norm_tables(n):
            slot, G = groups[n]
            nb = n % 2
            P.op(DVE, lambda e: e.tensor_copy(out=tA, in_=posi), reads=[Bpos], writes=[BtA])
            P.op(DVE, lambda e: e.tensor_scalar(out=tA, in0=tA, scalar1=invf, scalar2=None, op0=ALU.mult), reads=[BtA], writes=[BtA])
            ki = posi
            P.op(DVE, lambda e: e.tensor_scalar(out=ki, in0=tA, scalar1=float(1.0 / (2 * np.pi)), scalar2=None, op0=ALU.mult),
                 reads=[BtA], writes=[Bpos])
            P.op(DVE, lambda e: e.tensor_copy(out=tB, in_=ki), reads=[Bpos], writes=[BtB])
            C1 = 6.28125
            C2 = float(np.float32(2 * np.pi - 6.28125))
            P.op(DVE, lambda e: e.scalar_tensor_tensor(out=tA, in0=tB, scalar=-C1, in1=tA, op0=ALU.mult, op1=ALU.add), reads=[BtB, BtA], writes=[BtA])
            P.op(DVE, lambda e: e.scalar_tensor_tensor(out=tA, in0=tB, scalar=-C2, in1=tA, op0=ALU.mult, op1=ALU.add), reads=[BtB, BtA], writes=[BtA])
            PI_LO = 3.1415925
            P.op(DVE, lambda e: e.tensor_scalar(out=tA, in0=tA, scalar1=PI_LO, scalar2=-PI_LO, op0=ALU.min, op1=ALU.max), reads=[BtA], writes=[BtA])
            P.op(ACT, lambda e: e.activation(out=sin2[nb], in_=tA, func=AF.Sin, scale=sgn), reads=[BtA], writes=[Btab2[nb]])
            P.op(DVE, lambda e: e.tensor_scalar(out=tB, in0=tA, scalar1=float(np.pi / 2), scalar2=float(np.pi), op0=ALU.add, op1=ALU.is_gt),
                 reads=[BtA], writes=[BtB])
            P.op(DVE, lambda e: e.scalar_tensor_tensor(out=tB, in0=tB, scalar=-float(2 * np.pi), in1=tA, op0=ALU.mult, op1=ALU.add),
                 reads=[BtB, BtA], writes=[BtB])
            P.op(DVE, lambda e: e.tensor_scalar(out=tB, in0=tB, scalar1=float(np.pi / 2), scalar2=PI_LO, op0=ALU.add, op1=ALU.min), reads=[BtB], writes=[BtB])
            P.op(DVE, lambda e: e.tensor_scalar(out=tB, in0=tB, scalar1=-PI_LO, scalar2=None, op0=ALU.max), reads=[BtB], writes=[BtB])
            P.op(ACT, lambda e: e.activation(out=cos2[nb], in_=tB, func=AF.Sin), reads=[BtB], writes=[Btab2[nb]])

        def proj(n):
            slot, G = groups[n]
            nb = n % 2
            full = slot == 0
            hTg, BhT = hTg2[nb], BhT2[nb]
            tg = slice(G * 512, (G + 1) * 512)
            kvdst = kv_in if USE_CC else kv_all[slot * 128:(slot + 1) * 128, :]
            more = n + 1 < len(groups)
            if more:
                load_pos(n + 1)

            def vdiff(tts):
                for tt in tts:
                    b = next_bank()
                    for c in range(8):
                        P.op(PE, lambda e, c=c, b=b, tt=tt: e.matmul(bank(b), lhsT=hTg[:, c, tt * 128:(tt + 1) * 128], rhs=Wdv[:, c, :],
                                                                     start=(c == 0), stop=(c == 7)),
                             reads=[BW, BhT], writes=[psb[b]] if c in (0, 7) else [], signal=(c == 7))
                    P.op(ACT, lambda e, b=b, tt=tt: e.activation(out=Vd[:, tt, :, 0:128], in_=bank(b).rearrange("p (h j) -> p h j", h=4), func=AF.Copy),
                         reads=[psb[b]], writes=[BVd])

            for j in range(2):
                b = proj_fm(Wfm, 8, (12 + j) * 128, hTg, BhT)
                P.op(ACT, lambda e, j=j, b=b: e.activation(out=ckvraw[:, j, :], in_=bank(b), func=AF.Copy), reads=[psb[b]], writes=[Bckv])
                P.op(ACT, lambda e, j=j, b=b: e.activation(out=sq2[:, 3 + j, :], in_=bank(b), func=AF.Square), reads=[psb[b]], writes=[Bsq2])
            if full:
                for j in range(3):
                    b = proj_fm(Wfm, 8, (9 + j) * 128, hTg, BhT)
                    P.op(ACT, lambda e, j=j, b=b: e.activation(out=cqraw[:, j, :], in_=bank(b), func=AF.Copy), reads=[psb[b]], writes=[Bcq])
                    P.op(ACT, lambda e, j=j, b=b: e.activation(out=sq2[:, j, :], in_=bank(b), func=AF.Square), reads=[psb[b]], writes=[Bsq2])
            specs = [(Wfm, 8, (4 + h) * 128, hTg, BhT, kTd[:, h, :], BkTd) for h in range(4)]
            specs.append((Wfm, 8, 8 * 128, hTg, BhT, kTr, BkTr))
            rope_tiles(specs[0:2], nb)
            if more:
                norm(n + 1, part=1)
            b = next_bank()
            for j in range(2):
                P.op(PE, lambda e, j=j, b=b: e.matmul(bank(b), lhsT=ones, rhs=sq2[:, 3 + j, :], start=(j == 0), stop=(j == 1)),
                     reads=[Bsq2], writes=[psb[b]] if j in (0, 1) else [], signal=(j == 1))
            rstd_from_psum(b, 256.0, rstdkv, Brkv)
            for j in range(2):
                P.op(DVE, lambda e, j=j: e.scalar_tensor_tensor(out=ckvn[:, j, :], in0=ckvraw[:, j, :], scalar=g_kv[:, j:j + 1], in1=rstdkv,
                                                                op0=ALU.mult, op1=ALU.mult), reads=[Bckv, Brkv], writes=[Bckvn])
            if full:
                b = next_bank()
                for j in range(3):
                    P.op(PE, lambda e, j=j, b=b: e.matmul(bank(b), lhsT=ones, rhs=sq2[:, j, :], start=(j == 0), stop=(j == 2)),
                         reads=[Bsq2], writes=[psb[b]] if j in (0, 2) else [], signal=(j == 2))
                rstd_from_psum(b, 384.0, rstdq, Brq)
                for j in range(3):
                    P.op(DVE, lambda e, j=j: e.scalar_tensor_tensor(out=cqn[:, j, :], in0=cqraw[:, j, :], scalar=g_q[:, j:j + 1], in1=rstdq,
                                                                    op0=ALU.mult, op1=ALU.mult), reads=[Bcq, Brq], writes=[Bcqn])
            rope_tiles(specs[2:5], nb)
            P.dma(SP, kvin_k(kvdst, KD)[:, :, tg], kTd, reads=[BkTd], writes=[Bkvin], key="kvin_kTd")
            P.dma(SP, kvdst[:, KR + G * 512:KR + (G + 1) * 512], kTr, reads=[BkTr], writes=[Bkvin], key="kvin_kTr")
            if more:
                norm(n + 1, part=2)
            vdiff([0, 1])
            if full:
                rope_tiles([(Wfm, 8, h * 128, hTg, BhT, QTd[:, h, tg], BQ) for h in range(4)], nb)
            vdiff([2, 3])
            for tt in range(4):
                P.dma(SP, kvin_v(kvdst, VD)[:, 4 * G + tt, :, :], Vd[:, tt, :, :], reads=[BVd], writes=[Bkvin], key="kvin_Vd")
            if full:
                for h in range(4):
                    b = proj_fm(Wuq, 3, h * 128, cqn, Bcqn)
                    P.op(ACT, lambda e, h=h, b=b, tg=tg: e.activation(out=QTn[:, h, tg], in_=bank(b), func=AF.Copy), reads=[psb[b]], writes=[BQ])
                rope_tiles([(Wuq, 3, (4 + pr) * 128, cqn, Bcqn, QTr[:, pr, tg], BQ) for pr in range(2)], nb)
            for h in range(4):
                b = proj_fm(Wukv, 2, h * 128, ckvn, Bckvn)
                P.op(ACT, lambda e, h=h, b=b: e.activation(out=kTn[:, h, :], in_=bank(b), func=AF.Copy), reads=[psb[b]], writes=[BkTn])
            P.dma(SP, kvin_k(kvdst, KN)[:, :, tg], kTn, reads=[BkTn], writes=[Bkvin], key="kvin_kTn")
            for tt in range(4):
                b = next_bank()
                for j in range(2):
                    P.op(PE, lambda e, j=j, b=b, tt=tt: e.matmul(bank(b), lhsT=ckvn[:, j, tt * 128:(tt + 1) * 128], rhs=Wukv[:, j, 512:1024],
                                                                 start=(j == 0), stop=(j == 1)),
                         reads=[BW, Bckvn], writes=[psb[b]] if j in (0, 1) else [], signal=(j == 1))
                P.op(DVE, lambda e, b=b, tt=tt: e.tensor_copy(out=Vm[:, tt, :, 0:128], in_=bank(b).rearrange("p (h j) -> p h j", h=4)),
                     reads=[psb[b]], writes=[BVm])
            for tt in range(4):
                P.dma(SP, kvin_v(kvdst, VM)[:, 4 * G + tt, :, :], Vm[:, tt, :, :], reads=[BVm], writes=[Bkvin], key="kvin_Vm")
            if more:
                norm(n + 1, part=3)

        load_x(0)
        load_pos(0)
        norm(0)
        for n in range(len(groups)):
            proj(n)

        P.barrier()
        if stop == 1:
            P.dma(SP, dbg["kv"], kv_in if USE_CC else kv_all[0:128, :], writes=[Buf("dbgkv")], key="dbg")
            P.dma(SP, dbg["qt"][:, 0:4 * NT], QTd.rearrange("p h t -> p (h t)"), reads=[BQ], writes=[Buf("dbgq")], key="dbg")
            P.dma(SP, dbg["qt"][:, 4 * NT:8 * NT], QTn.rearrange("p h t -> p (h t)"), reads=[BQ], writes=[Buf("dbgq")], key="dbg")
            P.dma(SP, dbg["qt"][:, 8 * NT:10 * NT], QTr.rearrange("p h t -> p (h t)"), reads=[BQ], writes=[Buf("dbgq")], key="dbg")
            P.barrier()
            P.emit()
            return nc
        if USE_CC:
            P.raw(POOL, lambda e: e.collective_compute("AllGather", ALU.bypass, replica_groups=[list(range(NCORES))],
                                                       ins=[kv_in], outs=[kv_all]).then_inc(cc_sem, 1))
            for eng in ENGS:
                P.raw(eng, lambda e: e.wait_ge(cc_sem, 1))
        XP = os.environ.get("XP", "")
        if "a" in XP:
            P.barrier()
        if "b" in XP:
            scr = nc.dram_tensor("scr", [128, KVCOLS], BF16, kind="Internal").ap()
            P.dma(SP, scr, kv_in, writes=[Buf("scr")], key="scr")
            P.barrier()
        if debug and "B" in os.environ.get("DBG", "BC"):
            P.dma(SP, dbg["kv"], kv_in if USE_CC else kv_all[0:128, :], writes=[Buf("dbgkv")], key="dbg")
            P.dma(SP, dbg["qt"][:, 0:4 * NT], QTd.rearrange("p h t -> p (h t)"), reads=[BQ], writes=[Buf("dbgq")], key="dbg")
            P.dma(SP, dbg["qt"][:, 4 * NT:8 * NT], QTn.rearrange("p h t -> p (h t)"), reads=[BQ], writes=[Buf("dbgq")], key="dbg")
            P.dma(SP, dbg["qt"][:, 8 * NT:10 * NT], QTr.rearrange("p h t -> p (h t)"), reads=[BQ], writes=[Buf("dbgq")], key="dbg")
            P.barrier()

        A.cur = X0
        oT = A.alloc(8 * NT, BF16).rearrange("p (h t) -> p h t", h=8)
        BoT = Buf("oT")
        X1 = A.cur
        NKB = 3
        KTc = [A.alloc(8 * 512, BF16).rearrange("p (r n) -> p r n", r=8) for _ in range(NKB)]
        KRc = [A.alloc(8 * 512, BF16).rearrange("p (r n) -> p r n", r=8) for _ in range(NKB)]
        Vc = [A.alloc(8 * 4 * 129, BF16).rearrange("p (r n) -> p r n", r=8) for _ in range(NKB)]
        BKV = [Buf("kvc%d" % i) for i in range(NKB)]
        for i in range(NKB):
            P.op(POOL, lambda e, i=i: e.memset(KTc[i][64:128, :, :], 0.0), writes=[BKV[i]])
            P.op(POOL, lambda e, i=i: e.memset(KRc[i][0:64, :, :], 0.0), writes=[BKV[i]])
        NPT = 4
        PT = [A.alloc(1024, BF16) for _ in range(NPT)]
        BPT = [Buf("pt%d" % i) for i in range(NPT)]
        NPS = 2
        PSm = [A.alloc(1024, BF16) for _ in range(NPS)]
        BPSm = [Buf("psm%d" % i) for i in range(NPS)]
        Osb = A.alloc(4 * 512, F32)
        BOsb = Buf("Osb")
        ofin = A.alloc(512, F32)
        osq = A.alloc(512, BF16)
        rsb = A.alloc(512, F32)
        cm1 = A.alloc(1024, F32)
        cmh = A.alloc(512, F32)
        Bofin, Bosq, Brsb = Buf("ofin"), Buf("osq"), Buf("rsb")
        P.op(POOL, lambda e: e.memset(cm1, -1.0), writes=[Bc])
        P.op(POOL, lambda e: e.memset(cmh, -0.5), writes=[Bc])
        BO = Buf("Oacc")
        BS = [Buf("S0"), Buf("S1")]
        kvall_v = kv_all.rearrange("(r p) n -> p r n", p=128)
        NS = 8

        def Sslot(s):
            return ps[:, 2048 + s * 1024:2048 + (s + 1) * 1024]

        jobs = []
        for h in range(4):
            for u in range(4):
                for ci in range(u + 1):
                    jobs.append(("d", h, u, ci))
        for h in range(4):
            for p_ in range(2):
                for ci in range(2 * p_ + 2):
                    jobs.append(("m", h, p_, ci))

        def load_chunk(ji):
            kind, h, _, ci = jobs[ji]
            s = ji % NKB
            g0 = 4 * ci
            if kind == "d":
                P.dma(SP, KTc[s][0:64, :, :], kvall_v[0:64, :, KD + h * NT + g0 * 128:KD + h * NT + g0 * 128 + 512], writes=[BKV[s]])
                P.dma(SP, KRc[s][64:128, :, :], kvall_v[64:128, :, KD + h * NT + g0 * 128:KD + h * NT + g0 * 128 + 512], writes=[BKV[s]])
                P.dma(SP, Vc[s], kvall_v[:, :, VD + (h * TPC + g0) * 129:VD + (h * TPC + g0 + 4) * 129], writes=[BKV[s]])
            else:
                P.dma(SP, KTc[s], kvall_v[:, :, KN + h * NT + g0 * 128:KN + h * NT + g0 * 128 + 512], writes=[BKV[s]])
                lo = (h % 2) * 64
                zl = 64 - lo
                P.op(POOL, lambda e, s=s, zl=zl: e.memset(KRc[s][zl:zl + 64, :, :], 0.0), writes=[BKV[s]])
                P.dma(SP, KRc[s][lo:lo + 64, :, :], kvall_v[lo:lo + 64, :, KR + g0 * 128:KR + g0 * 128 + 512], writes=[BKV[s]])
                P.dma(SP, Vc[s], kvall_v[:, :, VM + (h * TPC + g0) * 129:VM + (h * TPC + g0 + 4) * 129], writes=[BKV[s]])

        steps = []
        for ji, (kind, h, qp, ci) in enumerate(jobs):
            nq = 4 if kind == "d" else 8
            mbase = nq * qp
            for gg in range(4):
                g = 4 * ci + gg
                if g > mbase + nq - 1:
                    continue
                for r in range(8):
                    steps.append((ji, kind, h, qp, g, gg, r))
        last_step_of_pass = {}
        for t, st in enumerate(steps):
            last_step_of_pass[(st[1], st[2], st[3])] = t
        nsteps = len(steps)
        slot_of = {}
        sctr = [0]

        def take_slot():
            s = sctr[0] % 2
            sctr[0] += 1
            return s

        SCALE_D = 64 ** -0.5
        SCALE_M = 192 ** -0.5

        def geom(t):
            ji, kind, h, qp, g, gg, r = steps[t]
            nq = 4 if kind == "d" else 8
            mbase = nq * qp
            m_lo = max(mbase, g)
            cols = (mbase + nq - m_lo) * 128
            c_lo = (m_lo - mbase) * 128
            return nq, mbase, m_lo, cols, c_lo

        def issue_qk(t):
            ji, kind, h, qp, g, gg, r = steps[t]
            nq, mbase, m_lo, cols, c_lo = geom(t)
            s = ji % NKB
            sl = take_slot()
            slot_of[t] = sl
            diag = g >= mbase
            q0 = m_lo * 128
            Sv = Sslot(sl)
            kcols = slice(gg * 128, (gg + 1) * 128)
            segs = []
            c0 = 0
            if kind == "d":
                segs.append((0, cols, diag))
            else:
                while c0 < cols:
                    c1 = min(cols, (c0 // 512 + 1) * 512)
                    segs.append((c0, c1, diag and c0 == 0))
                    c0 = c1
            mm = []
            if kind == "d":
                for mp in range(2):
                    kbuf = KTc[s] if mp == 0 else KRc[s]
                    for (a, b_, dg) in segs:
                        mm.append((Sv[:, mp * 512 + a:mp * 512 + b_], kbuf[:, r, kcols], QTd[:, h, q0 + a:q0 + b_], True, not dg))
                        if dg:
                            mm.append((Sv[:, mp * 512:mp * 512 + 128], ident, maskb[:, r, :], False, True))
            else:
                rows = slice((h % 2) * 64, (h % 2) * 64 + 64)
                for (a, b_, dg) in segs:
                    mm.append((Sv[:, a:b_], KTc[s][:, r, kcols], QTn[:, h, q0 + a:q0 + b_], True, False))
                    mm.append((Sv[:, a:b_], KRc[s][:, r, kcols], QTr[:, h // 2, q0 + a:q0 + b_], False, not dg))
                    if dg:
                        mm.append((Sv[:, 0:128], ident, maskb[:, r, :], False, True))
            for i, (o_, l_, r_, st_, sp_) in enumerate(mm):
                last = i == len(mm) - 1
                P.op(PE, lambda e, o_=o_, l_=l_, r_=r_, st_=st_, sp_=sp_: e.matmul(o_, lhsT=l_, rhs=r_, start=st_, stop=sp_, skip_group_check=True),
                     reads=[BKV[s], BQ], writes=[BS[sl]] if (i == 0 or last) else [], signal=last)

        def pt_view(buf, kind, cols):
            if kind == "d":
                return buf.rearrange("p (a n) -> p a n", a=2)[:, :, 0:cols]
            return buf[:, 0:cols]

        def issue_exp(t):
            ji, kind, h, qp, g, gg, r = steps[t]
            nq, mbase, m_lo, cols, c_lo = geom(t)
            sl = slot_of[t]
            pt = t % NPT
            src = pt_view(Sslot(sl), kind, cols)
            dst = pt_view(PT[pt], kind, cols)
            sc = SCALE_D if kind == "d" else SCALE_M
            P.op(ACT, lambda e: e.activation(out=dst, in_=src, func=AF.Exp, scale=sc), reads=[BS[sl]], writes=[BPT[pt]])

        def out_segs(kind, c_lo, cols):
            if kind == "d":
                return [(mp, c_lo, c_lo + cols, mp * 512) for mp in range(2)]
            res = []
            a = c_lo
            while a < c_lo + cols:
                b_ = min(c_lo + cols, (a // 512 + 1) * 512)
                res.append((a // 512, a % 512, a % 512 + (b_ - a), a - c_lo))
                a = b_
            return res

        def issue_pv(t):
            ji, kind, h, qp, g, gg, r = steps[t]
            nq, mbase, m_lo, cols, c_lo = geom(t)
            s = ji % NKB
            pt = t % NPT
            first = (g == 0 and r == 0)
            vt = Vc[s][:, r, gg * 129:gg * 129 + 128]
            pieces = out_segs(kind, c_lo, cols)
            for i, (bk, a, b_, po) in enumerate(pieces):
                o_ = ps[:, bk * 512 + a:bk * 512 + b_]
                r_ = PT[pt][:, po:po + (b_ - a)]
                P.op(PE, lambda e, o_=o_, r_=r_, first=first: e.matmul(o_, lhsT=vt, rhs=r_, start=first, stop=False, skip_group_check=True),
                     reads=[BPT[pt], BKV[s]], writes=[BO] if i == 0 else [], signal=False)

        def issue_preadd(t):
            ji, kind, h, qp, g, gg, r = steps[t]
            nq, mbase, m_lo, cols, c_lo = geom(t)
            k = r % NS
            if k == 0:
                return
            pi = (t // NS) % NPS
            dst = pt_view(PSm[pi], kind, cols)
            a_ = pt_view(PT[(t - 1) % NPT], kind, cols) if k == 1 else dst
            b_ = pt_view(PT[t % NPT], kind, cols)
            rd = [BPT[t % NPT]] + ([BPT[(t - 1) % NPT]] if k == 1 else [BPSm[pi]])
            P.op(DVE, lambda e: e.tensor_tensor(out=dst, in0=a_, in1=b_, op=ALU.add), reads=rd, writes=[BPSm[pi]])

        def issue_den(t):
            ji, kind, h, qp, g, gg, r = steps[t]
            if r % NS != NS - 1:
                return
            nq, mbase, m_lo, cols, c_lo = geom(t)
            pi = (t // NS) % NPS
            first = (g == 0 and r == NS - 1)
            is_last = last_step_of_pass[(kind, h, qp)] == t
            pieces = out_segs(kind, c_lo, cols)
            for i, (bk, a, b_, po) in enumerate(pieces):
                o_ = ps[:, (2 + bk) * 512 + a:(2 + bk) * 512 + b_]
                r_ = PSm[pi][:, po:po + (b_ - a)]
                last = is_last and i == len(pieces) - 1
                P.op(PE, lambda e, o_=o_, r_=r_, first=first: e.matmul(o_, lhsT=ones, rhs=r_, start=first, stop=False, skip_group_check=True),
                     reads=[BPSm[pi]], writes=[BO] if (i == 0 or last) else [], signal=last)

        deferred = []

        def finish_pass(kind, h, qp, t):
            nq = 4 if kind == "d" else 8
            q0 = nq * qp * 128
            P.op(DVE, lambda e: e.tensor_copy(out=Osb[:, 0:1024], in_=ps[:, 0:1024]), reads=[BO], writes=[BOsb])
            P.op(ACT, lambda e: e.activation(out=Osb[:, 1024:2048], in_=ps[:, 1024:2048], func=AF.Ln), reads=[BO], writes=[BOsb])
            P.op(ACT, lambda e: e.activation(out=Osb[:, 1024:2048], in_=Osb[:, 1024:2048], func=AF.Exp, scale=-1.0), reads=[BOsb], writes=[BOsb])
            if kind == "m":
                P.op(POOL, lambda e: e.tensor_tensor(out=oT[:, 4 + h, q0:q0 + 1024], in0=Osb[:, 0:1024], in1=Osb[:, 1024:2048], op=ALU.mult),
                     reads=[BOsb], writes=[BoT])
                return
            P.op(POOL, lambda e: e.tensor_tensor(out=Osb[:, 0:1024], in0=Osb[:, 0:1024], in1=Osb[:, 1024:2048], op=ALU.mult), reads=[BOsb], writes=[BOsb])
            P.op(POOL, lambda e: e.tensor_scalar(out=Osb[:, 512:1024], in0=Osb[:, 512:1024], scalar1=neglam, scalar2=None, op0=ALU.mult),
                 reads=[BOsb], writes=[BOsb])
            P.op(POOL, lambda e: e.tensor_tensor(out=ofin, in0=Osb[:, 0:512], in1=Osb[:, 512:1024], op=ALU.add), reads=[BOsb], writes=[Bofin])
            P.op(POOL, lambda e: e.tensor_tensor(out=osq, in0=ofin, in1=ofin, op=ALU.mult), reads=[Bofin], writes=[Bosq])

            def part_b(h=h, q0=q0):
                sl = take_slot()
                P.op(PE, lambda e: e.matmul(Sslot(sl)[:, 0:512], lhsT=ones, rhs=osq, start=True, stop=True), reads=[Bosq], writes=[BS[sl]])
                P.op(ACT, lambda e: e.activation(out=rsb, in_=Sslot(sl)[:, 0:512], func=AF.Ln, scale=1.0 / 128, bias=epsb),
                     reads=[BS[sl]], writes=[Brsb])
                P.op(ACT, lambda e: e.activation(out=rsb, in_=rsb, func=AF.Exp, scale=-0.5), reads=[Brsb], writes=[Brsb])
                P.op(POOL, lambda e: e.tensor_scalar(out=ofin, in0=ofin, scalar1=sublnc, scalar2=None, op0=ALU.mult), reads=[Bofin], writes=[Bofin])
                P.op(POOL, lambda e: e.tensor_tensor(out=oT[:, h, q0:q0 + 512], in0=ofin, in1=rsb, op=ALU.mult), reads=[Bofin, Brsb], writes=[BoT])
            deferred.append((t + 16, part_b))

        loaded = -1

        def ensure_loaded(upto):
            nonlocal loaded
            while loaded < min(upto, len(jobs) - 1):
                loaded += 1
                load_chunk(loaded)

        ensure_loaded(1)
        issue_qk(0)
        issue_exp(0)
        for t in range(nsteps):
            if t + 1 < nsteps:
                issue_qk(t + 1)
                issue_exp(t + 1)
            ensure_loaded(steps[t][0] + NKB - 1)
            prev_last = t > 0 and last_step_of_pass[(steps[t - 1][1], steps[t - 1][2], steps[t - 1][3])] == t - 1
            if prev_last:
                issue_den(t - 1)
                finish_pass(steps[t - 1][1], steps[t - 1][2], steps[t - 1][3], t - 1)
            issue_pv(t)
            issue_preadd(t)
            if t > 0 and not prev_last:
                issue_den(t - 1)
            while deferred and deferred[0][0] <= t:
                deferred.pop(0)[1]()
        issue_den(nsteps - 1)
        finish_pass(steps[-1][1], steps[-1][2], steps[-1][3], nsteps - 1)
        while deferred:
            deferred.pop(0)[1]()

        P.barrier()
        if debug and "C" in os.environ.get("DBG", "BC"):
            P.dma(SP, dbg["ot"], oT.rearrange("p h t -> p (h t)"), reads=[BoT], writes=[Buf("dbgo")], key="dbg")
            P.barrier()
        if stop == 2:
            P.emit()
            return nc

        A.cur = X1
        Wg3 = A.alloc(8 * 3072, BF16).rearrange("p (c n) -> p c n", c=8)
        Wpd = A.alloc(4 * DM, BF16).rearrange("p (c n) -> p c n", c=4)
        Wpm = A.alloc(4 * DM, BF16).rearrange("p (c n) -> p c n", c=4)
        Wout = A.alloc(8 * DM, BF16).rearrange("p (c n) -> p c n", c=8)
        hT3b = [A.alloc(8 * 512, BF16).rearrange("p (c n) -> p c n", c=8) for _ in range(2)]
        oTg = A.alloc(8 * 512, BF16).rearrange("p (c n) -> p c n", c=8)
        mT = A.alloc(8 * 512, BF16).rearrange("p (c n) -> p c n", c=8)
        P3B_END = A.cur
        A.cur = CONST_END
        sg = [A.alloc(512, F32) for _ in range(2)]
        sgd = [A.alloc(512, F32) for _ in range(2)]
        sgm = [A.alloc(512, F32) for _ in range(2)]
        m1 = A.alloc(512, F32)
        m2 = A.alloc(512, F32)
        xt = [A.alloc(DM, F32) for _ in range(2)]
        yb = [A.alloc(DM, F32) for _ in range(2)]
        junk = A.alloc(DM, F32)
        sm3 = A.alloc(16, F32)
        assert A.cur <= X0, (A.cur, X0)
        BWs, BWp, BWg, BWo = Buf("w3s"), Buf("w3p"), Buf("w3g"), Buf("w3o")
        wg3_v = wg3.rearrange("(c p) n -> p c n", p=128)
        for c in range(8):
            P.dma(POOL, Wg3[:, c, 0:1024], wg3_v[:, c, 0:1024], writes=[BWs], key="w3s")
        P.dma(POOL, Wpd, wpd.rearrange("(c p) n -> p c n", p=128), writes=[BWp], key="w3p")
        P.dma(POOL, Wpm, wpm.rearrange("(c p) n -> p c n", p=128), writes=[BWp], key="w3p")
        for c in range(8):
            P.dma(POOL, Wg3[:, c, 1024:3072], wg3_v[:, c, 1024:3072], writes=[BWg], key="w3g")
        wout_v = wout.rearrange("(c p) n -> p c n", p=128)
        for c in range(0, 8, 2):
            P.dma(POOL, Wout[:, c:c + 2, :], wout_v[:, c:c + 2, :], writes=[BWo], key="w3o")
        BhT3b, BoTg, BmT = [Buf("hT3a"), Buf("hT3b")], Buf("oTg"), Buf("mT")
        Bsg = [Buf("sg0"), Buf("sg1")]
        Bsgd = [Buf("sgd0"), Buf("sgd1")]
        Bsgm = [Buf("sgm0"), Buf("sgm1")]
        Bm1, Bm2 = Buf("m1"), Buf("m2")
        Bxt = [Buf("xt0"), Buf("xt1")]
        Byb = [Buf("yb0"), Buf("yb1")]
        Bjunk, Bsm3 = Buf("junk"), Buf("sm3")
        By = Buf("y")
        hTd_v = hT_d.rearrange("p (c t) -> p c t", c=8)
        ti = 0
        P.dma(SP, hT3b[0], hTd_v[:, :, 0:512], reads=[BhTd], writes=[BhT3b[0]])
        for G in range(4):
            tg = slice(G * 512, (G + 1) * 512)
            hT3, BhT3 = hT3b[G % 2], BhT3b[G % 2]
            if G + 1 < 4:
                P.dma(SP, hT3b[(G + 1) % 2], hTd_v[:, :, (G + 1) * 512:(G + 2) * 512], reads=[BhTd], writes=[BhT3b[(G + 1) % 2]])
            for j in range(8):
                b = proj_fm(Wg3, 8, j * 128, hT3, BhT3, BWx=BWs)
                k = j % 2
                P.op(ACT, lambda e, b=b, k=k: e.activation(out=sg[k], in_=bank(b), func=AF.Silu), reads=[psb[b]], writes=[Bsg[k]])
                P.op(DVE, lambda e, j=j, k=k, tg=tg: e.tensor_tensor(out=oTg[:, j, :], in0=oT[:, j, tg], in1=sg[k], op=ALU.mult),
                     reads=[Bsg[k], BoT], writes=[BoTg])
            for n in range(8):
                k = n % 2
                bd = next_bank()
                for hh in range(4):
                    P.op(PE, lambda e, hh=hh, bd=bd, n=n: e.matmul(bank(bd), lhsT=Wpd[:, hh, n * 128:(n + 1) * 128], rhs=oTg[:, hh, :],
                                                                  start=(hh == 0), stop=(hh == 3)),
                         reads=[BWp, BoTg], writes=[psb[bd]] if hh in (0, 3) else [], signal=(hh == 3))
                bm = next_bank()
                for hh in range(4):
                    P.op(PE, lambda e, hh=hh, bm=bm, n=n: e.matmul(bank(bm), lhsT=Wpm[:, hh, n * 128:(n + 1) * 128], rhs=oTg[:, 4 + hh, :],
                                                                  start=(hh == 0), stop=(hh == 3)),
                         reads=[BWp, BoTg], writes=[psb[bm]] if hh in (0, 3) else [], signal=(hh == 3))
                bgd = proj_fm(Wg3, 8, 1024 + n * 128, hT3, BhT3, BWx=BWg)
                bgm = proj_fm(Wg3, 8, 2048 + n * 128, hT3, BhT3, BWx=BWg)
                P.op(ACT, lambda e, b=bgd, k=k: e.activation(out=sgd[k], in_=bank(b), func=AF.Sigmoid), reads=[psb[bgd]], writes=[Bsgd[k]])
                P.op(ACT, lambda e, b=bgm, k=k: e.activation(out=sgm[k], in_=bank(b), func=AF.Sigmoid), reads=[psb[bgm]], writes=[Bsgm[k]])
                P.op(DVE, lambda e, bd=bd, k=k: e.tensor_tensor(out=m1, in0=bank(bd), in1=sgd[k], op=ALU.mult), reads=[psb[bd], Bsgd[k]], writes=[Bm1])
                P.op(DVE, lambda e, bm=bm, k=k: e.tensor_tensor(out=m2, in0=bank(bm), in1=sgm[k], op=ALU.mult), reads=[psb[bm], Bsgm[k]], writes=[Bm2])
                P.op(POOL, lambda e, n=n: e.tensor_tensor(out=mT[:, n, :], in0=m1, in1=m2, op=ALU.add), reads=[Bm1, Bm2], writes=[BmT])
            for tt in range(4):
                k = ti % 2
                ti += 1
                row0 = G * 512 + tt * 128
                P.dma(SP, xt[k], xtok[row0:row0 + 128, :], writes=[Bxt[k]])
                for hf in range(2):
                    b = next_bank()
                    for n in range(8):
                        P.op(PE, lambda e, n=n, b=b, tt=tt, hf=hf: e.matmul(bank(b), lhsT=mT[:, n, tt * 128:(tt + 1) * 128], rhs=Wout[:, n, hf * 512:(hf + 1) * 512],
                                                                       start=(n == 0), stop=(n == 7)),
                             reads=[BWo, BmT], writes=[psb[b]] if n in (0, 7) else [], signal=(n == 7))
                    P.op(DVE, lambda e, b=b, k=k, hf=hf: e.tensor_tensor(out=yb[k][:, hf * 512:(hf + 1) * 512], in0=bank(b), in1=xt[k][:, hf * 512:(hf + 1) * 512], op=ALU.add),
                         reads=[psb[b], Bxt[k]], writes=[Byb[k]])
                P.op(DVE, lambda e, k=k: e.scalar_tensor_tensor(out=junk, in0=yb[k], scalar=1.0, in1=yb[k], op0=ALU.mult, op1=ALU.mult, accum_out=sm3[:, 0:1]),
                     reads=[Byb[k], Bsm3], writes=[Bjunk, Bsm3])
                P.op(ACT, lambda e: e.activation(out=sm3[:, 1:2], in_=sm3[:, 0:1], func=AF.Sqrt, scale=1.0 / DM, bias=epsb), reads=[Bsm3], writes=[Bsm3])
                P.op(DVE, lambda e: e.reciprocal(out=sm3[:, 2:3], in_=sm3[:, 1:2]), reads=[Bsm3], writes=[Bsm3])
                P.op(DVE, lambda e, k=k: e.scalar_tensor_tensor(out=yb[k], in0=yb[k], scalar=sm3[:, 2:3], in1=nfb, op0=ALU.mult, op1=ALU.mult),
                     reads=[Byb[k], Bsm3, Bc], writes=[Byb[k]])
                P.dma(SP, y[row0:row0 + 128, :], yb[k], reads=[Byb[k]], writes=[By], key="yout%d" % k)
        P.barrier()
        P.emit()
    return nc


def _rot_cols(w):
    k, n = w.shape
    w4 = w.reshape(k, n // 64, 2, 32)
    return np.ascontiguousarray(w4[:, :, ::-1, :]).reshape(k, n)


def _prep_inputs(x, positions, norm_in, w_in, diff_lambda_q1, diff_lambda_k1, diff_lambda_q2, diff_lambda_k2,
                 diff_subln, mla_q_norm, w_uq, mla_kv_norm, w_ukv, w_proj_diff, w_proj_mla, w_out, norm_final):
    f32 = np.float32
    x = np.asarray(x, f32)[0]
    positions = np.asarray(positions)[0].astype(np.int32)
    w_in = np.asarray(w_in, f32)[0]
    o = 0
    sl = {}
    for name, n in (("dq", 512), ("dk", 512), ("dv", 512), ("dgate", 512), ("cq", 384), ("ckv", 256), ("kr", 64),
                    ("mgate", 512), ("gd", 1024), ("gm", 1024)):
        sl[name] = w_in[:, o:o + n]
        o += n
    kr2 = np.concatenate([sl["kr"], sl["kr"]], 1)
    wfm = np.concatenate([sl["dq"], sl["dk"], kr2, sl["cq"], sl["ckv"]], 1)
    assert wfm.shape[1] == NFM
    wg3 = np.concatenate([sl["dgate"], sl["mgate"], sl["gd"], sl["gm"]], 1)
    uq = np.asarray(w_uq, f32)[0].reshape(384, 4, 192)
    uq_n = uq[:, :, :128].reshape(384, 512)
    uq_r = uq[:, :, 128:].reshape(384, 256)
    wuq = np.concatenate([uq_n, uq_r], 1)
    ukv = np.asarray(w_ukv, f32)[0].reshape(256, 4, 256)
    wukv = np.concatenate([ukv[:, :, :128].reshape(256, 512), ukv[:, :, 128:].reshape(256, 512)], 1)
    cst = np.zeros((128, 16), f32)
    cst[:, 0:8] = np.asarray(norm_in, f32)[0].reshape(8, 128).T
    cst[:, 8:11] = np.asarray(mla_q_norm, f32)[0].reshape(3, 128).T
    cst[:, 11:13] = np.asarray(mla_kv_norm, f32)[0].reshape(2, 128).T
    inv_freq = (np.float32(10000.0) ** (-np.arange(0, 64, 2, dtype=np.float32) / np.float32(64))).astype(f32)
    pidx = np.arange(128)
    cst[:, 13] = inv_freq[pidx % 32]
    cst[:, 14] = np.where((pidx % 64) < 32, -1.0, 1.0)
    cst[:, 15] = np.asarray(diff_subln, f32).reshape(128)
    perm = np.zeros((128, 128), f32)
    perm[np.where((pidx % 64) < 32, pidx + 32, pidx - 32), pidx] = 1.0
    lamv = np.concatenate([np.asarray(a, f32)[0] for a in (diff_lambda_q1, diff_lambda_k1, diff_lambda_q2, diff_lambda_k2)])[None]
    shared = {
        "wfm": np.ascontiguousarray(wfm), "wdv": np.ascontiguousarray(sl["dv"]), "wuq": np.ascontiguousarray(wuq),
        "wukv": np.ascontiguousarray(wukv), "wg3": np.ascontiguousarray(wg3),
        "wpd": np.ascontiguousarray(np.asarray(w_proj_diff, f32)[0]), "wpm": np.ascontiguousarray(np.asarray(w_proj_mla, f32)[0]),
        "wout": np.ascontiguousarray(np.asarray(w_out, f32)[0]), "cst": cst, "lamv": np.ascontiguousarray(lamv),
        "subln": np.asarray(diff_subln, f32).reshape(1, 128), "nfin": np.asarray(norm_final, f32).reshape(1, DM),
        "ident": np.eye(128, dtype=f32), "perm": perm,
    }
    xt = x.reshape(TPC, NCORES, 128, DM)
    pt = positions.reshape(TPC, NCORES, 128)
    kk = np.arange(128)[:, None]
    qq = np.arange(128)[None, :]
    in_maps = []
    for c in range(NCORES):
        xc = np.ascontiguousarray(xt[:, c]).reshape(NT, DM)
        mask = np.zeros((128, 8, 128), f32)
        for j in range(8):
            r = j if USE_CC else (c + j) % NCORES
            if r > c:
                mask[:, j, :] = NEG
            elif r == c:
                mask[:, j, :] = np.where(kk > qq, NEG, 0.0)
        d = dict(shared)
        ranks = [(c + j) % NCORES for j in range(NSLOT)]
        xg = np.stack([xt[:, r].reshape(4, 512, 8, 128) for r in ranks])
        d["xT"] = np.ascontiguousarray(xg.transpose(0, 1, 4, 3, 2)).reshape(NSLOT * 4, 128, 8 * 512)
        d["xtok"] = xc
        d["pos"] = np.ascontiguousarray(np.concatenate([pt[:, r].reshape(NT) for r in ranks])).reshape(1, NSLOT * NT)
        d["mask"] = mask.reshape(128, 1024)
        in_maps.append(d)
    return in_maps


_NC_CACHE = {}


def kernel(**inputs):
    in_maps = _prep_inputs(**inputs)
    if "nc" not in _NC_CACHE:
        _NC_CACHE["nc"] = build_program(debug=bool(os.environ.get("DBG")))
    res = run_bass_kernel_spmd(_NC_CACHE["nc"], in_maps, core_ids=list(range(NCORES)))
    out = np.empty((TPC, NCORES, 128, DM), np.float32)
    for c in range(NCORES):
        out[:, c] = np.asarray(res.results[c]["y"], np.float32).reshape(TPC, 128, DM)
    return out.reshape(1, SEQ, DM)
```

```python
import contextlib
import math
import os
import numpy as np
import concourse.bass as bass
import concourse.mybir as mybir
from concourse.bass_utils import run_bass_kernel_spmd

F32 = mybir.dt.float32
BF16 = mybir.dt.bfloat16
I32 = mybir.dt.int32
AF = mybir.ActivationFunctionType
ALU = mybir.AluOpType

PE, ACT, DVE, POOL, SP = "tensor", "scalar", "vector", "gpsimd", "sync"
ENGS = (PE, ACT, DVE, POOL, SP)

NCORES = 8
SEQ = 16384
DM = 1024
NT = SEQ // NCORES
TPC = NT // 128
EPS = 1e-6
LAMBDA_INIT = 0.8 - 0.6 * math.exp(-0.3 * 0)
NEG = -30000.0

KD = 0
KN = KD + 4 * NT
KR = KN + 4 * NT
VD = KR + NT
VM = VD + 4 * TPC * 129
KVCOLS = VM + 4 * TPC * 129

NFM = 14 * 128
USE_CC = False
NSLOT = 1 if USE_CC else NCORES


class Tok:
    __slots__ = ("sem", "val")

    def __init__(self, sem, val=None):
        self.sem = sem
        self.val = val


class Buf:
    __slots__ = ("name", "w", "r")

    def __init__(self, name=""):
        self.name = name
        self.w = None
        self.r = []


class Prog:
    def __init__(self, nc, es):
        self.nc = nc
        self.es = es
        self.q = {e: [] for e in ENGS}
        self.esem = {e: es.enter_context(nc.semaphore("s_" + e)) for e in ENGS if e != SP}
        self.ecnt = {e: 0 for e in ENGS}
        self.pending = {e: [] for e in ENGS}
        self.seen = {e: {} for e in ENGS}
        self.dsem = {}
        self.dcnt = {}
        self.always = []

    def _waits(self, eng, reads, writes, exclude=None):
        toks = []
        for b in reads:
            if b.w is not None:
                toks.append(b.w)
        for b in writes:
            if b.w is not None:
                toks.append(b.w)
            toks.extend(b.r)
        need = {}
        own = self.esem.get(eng)
        for t in toks:
            if t.sem is exclude:
                continue
            if t.val is None and t.sem is own:
                continue
            assert t.val is not None, "wait on unresolved token (missing signal)"
            k = id(t.sem)
            if k not in need or need[k][1] < t.val:
                need[k] = (t.sem, t.val)
        out = []
        seen = self.seen[eng]
        for k, (sem, val) in need.items():
            if seen.get(k, 0) >= val:
                continue
            seen[k] = val
            out.append((sem, val))
        return out

    def op(self, eng, fn, reads=(), writes=(), signal=True):
        reads = list(reads) + self.always
        waits = self._waits(eng, reads, writes)
        if signal:
            self.ecnt[eng] += 1
            val = self.ecnt[eng]
            sem = self.esem[eng]
            for t in self.pending[eng]:
                t.val = val
            self.pending[eng] = []
            tok = Tok(sem, val)

            def run(e, fn=fn, waits=waits, sem=sem):
                for (s, v) in waits:
                    e.wait_ge(s, v)
                fn(e).then_inc(sem, 1)
        else:
            tok = Tok(self.esem[eng], None)
            self.pending[eng].append(tok)

            def run(e, fn=fn, waits=waits):
                for (s, v) in waits:
                    e.wait_ge(s, v)
                fn(e)
        self.q[eng].append(run)
        for b in writes:
            b.w = tok
            b.r = []
        for b in reads:
            b.r.append(tok)
        return tok

    def dma(self, queue, out, in_, reads=(), writes=(), key=None):
        if key is None:
            key = writes[0].name
        if key not in self.dsem:
            self.dsem[key] = self.es.enter_context(self.nc.semaphore("d_" + key))
            self.dcnt[key] = 0
        waits = self._waits(queue, reads, writes, exclude=self.dsem[key])
        self.dcnt[key] += 16
        sem = self.dsem[key]
        tok = Tok(sem, self.dcnt[key])

        def run(e, waits=waits, sem=sem, out=out, in_=in_):
            for (s, v) in waits:
                e.wait_ge(s, v)
            e.dma_start(out=out, in_=in_).then_inc(sem, 16)
        self.q[queue].append(run)
        for b in writes:
            b.w = tok
            b.r = []
        for b in reads:
            b.r.append(tok)
        return tok

    def raw(self, eng, fn):
        self.q[eng].append(fn)

    def all_tokens(self):
        toks = [Tok(self.esem[e], self.ecnt[e]) for e in self.esem if self.ecnt[e] > 0]
        toks += [Tok(self.dsem[k], self.dcnt[k]) for k in self.dsem]
        return toks

    def barrier(self, engines=ENGS):
        toks = self.all_tokens()
        for eng in engines:
            lst = []
            seen = self.seen[eng]
            for t in toks:
                k = id(t.sem)
                if seen.get(k, 0) >= t.val:
                    continue
                seen[k] = t.val
                lst.append((t.sem, t.val))

            def run(e, lst=lst):
                for (s, v) in lst:
                    e.wait_ge(s, v)
            self.q[eng].append(run)

    def emit(self):
        with self.nc.Block() as block:
            @block.tensor
            def _(e):
                for f in self.q[PE]:
                    f(e)

            @block.scalar
            def _(e):
                for f in self.q[ACT]:
                    f(e)

            @block.vector
            def _(e):
                for f in self.q[DVE]:
                    f(e)

            @block.gpsimd
            def _(e):
                for f in self.q[POOL]:
                    f(e)

            @block.sync
            def _(e):
                for f in self.q[SP]:
                    f(e)


class Arena:
    def __init__(self, ap, nbytes):
        self.ap = ap
        self.nbytes = nbytes
        self.cur = 0

    def at(self, off, n, dt):
        size = 2 if dt == BF16 else 4
        nb = (n * size + 31) // 32 * 32
        assert off % 4 == 0 and off + nb <= self.nbytes, ("arena overflow", off, nb, self.nbytes)
        v = self.ap[:, off // 4:(off + nb) // 4]
        if dt != F32:
            v = v.bitcast(dt)
        return v[:, 0:n], off + nb

    def alloc(self, n, dt):
        v, self.cur = self.at(self.cur, n, dt)
        return v


def build_program(debug=False, stop=0):
    nc = bass.Bass("TRN2", target_bir_lowering=False)

    def din(name, shape, dt=F32):
        return nc.dram_tensor(name, list(shape), dt, kind="ExternalInput").ap()

    xT = din("xT", [NSLOT * 4, 128, 8 * 512])
    xtok = din("xtok", [NT, DM])
    pos = din("pos", [1, NSLOT * NT], I32)
    wfm = din("wfm", [DM, NFM])
    wdv = din("wdv", [DM, 512])
    wuq = din("wuq", [384, 768])
    perm_d = din("perm", [128, 128])
    wukv = din("wukv", [256, 1024])
    wg3 = din("wg3", [DM, 3072])
    wpd = din("wpd", [512, DM])
    wpm = din("wpm", [512, DM])
    wout = din("wout", [DM, DM])
    cst = din("cst", [128, 16])
    lamv = din("lamv", [1, 256])
    subln = din("subln", [1, 128])
    nfin = din("nfin", [1, DM])
    ident_d = din("ident", [128, 128])
    mask_d = din("mask", [128, 1024])
    y = nc.dram_tensor("y", [NT, DM], F32, kind="ExternalOutput").ap()
    kv_in = nc.dram_tensor("kv_in", [128, KVCOLS], BF16, kind="Internal").ap()
    kv_all = nc.dram_tensor("kv_all", [128 * NCORES, KVCOLS], BF16, kind="Internal").ap()
    hT_d = nc.dram_tensor("hT_d", [128, 8 * NT], BF16, kind="Internal").ap()
    dbg = {}
    if debug:
        dbg["kv"] = nc.dram_tensor("dbg_kv", [128, KVCOLS], BF16, kind="ExternalOutput").ap()
        dbg["qt"] = nc.dram_tensor("dbg_qt", [128, 10 * NT], BF16, kind="ExternalOutput").ap()
        dbg["ot"] = nc.dram_tensor("dbg_ot", [128, 8 * NT], BF16, kind="ExternalOutput").ap()

    with contextlib.ExitStack() as es:
        P = Prog(nc, es)
        ARENA_BYTES = 211968
        arena_t = es.enter_context(nc.sbuf_tensor("arena", [128, ARENA_BYTES // 4], F32))
        A = Arena(arena_t[:], ARENA_BYTES)
        ps = es.enter_context(nc.psum_tensor("ps", [128, 4096], F32))
        cc_sem = es.enter_context(nc.semaphore("cc"))

        def bank(b):
            return ps[:, b * 512:(b + 1) * 512]

        psb = [Buf("psb%d" % i) for i in range(8)]
        rr = [0]

        def next_bank():
            b = rr[0] % 8
            rr[0] += 1
            return b

        ident = A.alloc(128, BF16)
        ones = A.alloc(128, BF16)
        maskb = A.alloc(1024, BF16).rearrange("p (r q) -> p r q", r=8)
        cstt = A.alloc(16, F32)
        lamt = A.alloc(256, F32)
        lamw = A.alloc(8, F32)
        sublnb = A.alloc(128, F32)
        nfb = A.alloc(DM, F32)
        half = A.alloc(8, F32)
        junk64 = A.alloc(64, F32)
        Bc = Buf("consts")

        def pbcast(ap):
            v = ap.partition_broadcast(128)
            if len(v.shape) == 3:
                v = v.rearrange("p o n -> p (o n)")
            return v

        P.dma(POOL, ident, ident_d, writes=[Bc], key="c0")
        P.dma(POOL, maskb.rearrange("p r q -> p (r q)"), mask_d, writes=[Bc], key="c0")
        P.dma(POOL, cstt, cst, writes=[Bc], key="c0")
        P.dma(POOL, lamt, pbcast(lamv), writes=[Bc], key="c0")
        P.dma(POOL, sublnb, pbcast(subln), writes=[Bc], key="c0")
        P.dma(POOL, nfb, pbcast(nfin), writes=[Bc], key="c0")
        P.op(DVE, lambda e: e.memset(ones, 1.0), writes=[Bc])
        P.op(DVE, lambda e: e.memset(half[:, 0:1], EPS), writes=[Bc])
        P.op(DVE, lambda e: e.memset(half[:, 1:2], 0.0), writes=[Bc])
        P.op(DVE, lambda e: e.memset(half[:, 2:3], float(np.pi / 2)), writes=[Bc])
        P.op(DVE, lambda e: e.memset(half[:, 3:4], -0.5), writes=[Bc])
        epsb = half[:, 0:1]
        pio2 = half[:, 2:3]
        mhalf = half[:, 3:4]
        P.op(DVE, lambda e: e.scalar_tensor_tensor(out=junk64, in0=lamt[:, 0:64], scalar=1.0, in1=lamt[:, 64:128],
                                                   op0=ALU.mult, op1=ALU.mult, accum_out=lamw[:, 0:1]),
             reads=[Bc], writes=[Bc])
        P.op(DVE, lambda e: e.scalar_tensor_tensor(out=junk64, in0=lamt[:, 128:192], scalar=1.0, in1=lamt[:, 192:256],
                                                   op0=ALU.mult, op1=ALU.mult, accum_out=lamw[:, 1:2]),
             reads=[Bc], writes=[Bc])
        P.op(ACT, lambda e: e.activation(out=lamw[:, 2:4], in_=lamw[:, 0:2], func=AF.Exp), reads=[Bc], writes=[Bc])
        P.op(DVE, lambda e: e.scalar_tensor_tensor(out=lamw[:, 4:5], in0=lamw[:, 3:4], scalar=-LAMBDA_INIT, in1=lamw[:, 2:3],
                                                   op0=ALU.add, op1=ALU.subtract), reads=[Bc], writes=[Bc])
        neglam = lamw[:, 4:5]
        Blam = Bc
        P.op(DVE, lambda e: e.tensor_scalar(out=sublnb, in0=sublnb, scalar1=1.0 - LAMBDA_INIT, scalar2=None, op0=ALU.mult),
             reads=[Bc], writes=[Bc])
        P.op(DVE, lambda e: e.tensor_scalar(out=cstt[:, 15:16], in0=cstt[:, 15:16], scalar1=1.0 - LAMBDA_INIT, scalar2=None, op0=ALU.mult),
             reads=[Bc], writes=[Bc])
        P.always = [Bc]
        g_in = cstt[:, 0:8]
        g_q = cstt[:, 8:11]
        g_kv = cstt[:, 11:13]
        invf = cstt[:, 13:14]
        sgn = cstt[:, 14:15]
        sublnc = cstt[:, 15:16]
        CONST_END = A.cur

        QTd = A.alloc(4 * NT, BF16).rearrange("p (h t) -> p h t", h=4)
        QTn = A.alloc(4 * NT, BF16).rearrange("p (h t) -> p h t", h=4)
        QTr = A.alloc(2 * NT, BF16).rearrange("p (h t) -> p h t", h=2)
        BQ = Buf("QT")
        X0 = A.cur

        Wfm = A.alloc(8 * NFM, BF16).rearrange("p (c n) -> p c n", c=8)
        Wdv = A.alloc(8 * 512, BF16).rearrange("p (c n) -> p c n", c=8)
        Wuq = A.alloc(3 * 768, BF16).rearrange("p (c n) -> p c n", c=3)
        Wukv = A.alloc(2 * 1024, BF16).rearrange("p (c n) -> p c n", c=2)
        permb = A.alloc(128, BF16)
        xTg = A.alloc(8 * 512, F32).rearrange("p (c n) -> p c n", c=8)
        sqg = A.alloc(8 * 512, BF16).rearrange("p (c n) -> p c n", c=8)
        sq2 = A.alloc(5 * 512, BF16).rearrange("p (c n) -> p c n", c=5)
        rstd2 = [A.alloc(512, F32) for _ in range(2)]
        hTg2 = [A.alloc(8 * 512, BF16).rearrange("p (c n) -> p c n", c=8) for _ in range(2)]
        cos2 = [A.alloc(512, F32) for _ in range(2)]
        sin2 = [A.alloc(512, F32) for _ in range(2)]
        tA = A.alloc(512, F32)
        tB = A.alloc(512, F32)
        posi = A.alloc(512, I32)
        mA2 = [A.alloc(512, F32) for _ in range(2)]
        mB2 = [A.alloc(512, F32) for _ in range(2)]
        rawbf = [A.alloc(512, BF16) for _ in range(2)]
        cqraw = A.alloc(3 * 512, F32).rearrange("p (c n) -> p c n", c=3)
        cqn = A.alloc(3 * 512, BF16).rearrange("p (c n) -> p c n", c=3)
        rstdq = A.alloc(512, F32)
        ckvraw = A.alloc(2 * 512, F32).rearrange("p (c n) -> p c n", c=2)
        ckvn = A.alloc(2 * 512, BF16).rearrange("p (c n) -> p c n", c=2)
        rstdkv = A.alloc(512, F32)
        kTd = A.alloc(4 * 512, BF16).rearrange("p (h n) -> p h n", h=4)
        kTn = A.alloc(4 * 512, BF16).rearrange("p (h n) -> p h n", h=4)
        kTr = A.alloc(512, BF16)
        Vd = A.alloc(4 * 4 * 129, BF16).rearrange("p (t h j) -> p t h j", t=4, h=4)
        Vm = A.alloc(4 * 4 * 129, BF16).rearrange("p (t h j) -> p t h j", t=4, h=4)
        P1_END = A.cur

        BW = Buf("w1")
        wfm_v = wfm.rearrange("(c p) n -> p c n", p=128)
        P.dma(POOL, permb, perm_d, writes=[BW], key="w1")
        for c in range(8):
            P.dma(POOL, Wfm[:, c, :], wfm_v[:, c, :], writes=[BW], key="w1")
        P.dma(POOL, Wdv, wdv.rearrange("(c p) n -> p c n", p=128), writes=[BW], key="w1")
        P.dma(POOL, Wuq, wuq.rearrange("(c p) n -> p c n", p=128), writes=[BW], key="w1")
        P.dma(POOL, Wukv, wukv.rearrange("(c p) n -> p c n", p=128), writes=[BW], key="w1")

        BxT, Bsq, Bsq2, BtA, BtB, Bpos = [Buf(n) for n in "xT sq sq2 tA tB posi".split()]
        Brstd2 = [Buf("rstd0"), Buf("rstd1")]
        BhT2 = [Buf("hT0"), Buf("hT1")]
        Btab2 = [Buf("tab0"), Buf("tab1")]
        BmA2 = [Buf("mA0"), Buf("mA1")]
        BmB2 = [Buf("mB0"), Buf("mB1")]
        Braw = [Buf("raw0"), Buf("raw1")]
        Bcq, Bcqn, Brq, Bckv, Bckvn, Brkv = [Buf(n) for n in "cq cqn rq ckv ckvn rkv".split()]
        BkTd, BkTn, BkTr, BVd, BVm = [Buf(n) for n in "kTd kTn kTr Vd Vm".split()]
        Bkvin = Buf("kvin")
        BhTd = Buf("hTd")
        for tt in range(4):
            P.op(POOL, lambda e, tt=tt: e.memset(Vd[:, tt, :, 128:129], 1.0), writes=[BVd])
            P.op(POOL, lambda e, tt=tt: e.memset(Vm[:, tt, :, 128:129], 1.0), writes=[BVm])

        kvin_k = lambda dst, base: dst[:, base:base + 4 * NT].rearrange("p (h t) -> p h t", h=4)
        kvin_v = lambda dst, base: dst[:, base:base + 4 * TPC * 129].rearrange("p (h m j) -> p m h j", h=4, m=TPC)

        def rstd_from_psum(b, nfeat, rdst, Brd):
            if "s" in os.environ.get("XP", ""):
                P.op(ACT, lambda e: e.activation(out=rdst, in_=bank(b), func=AF.Sqrt, scale=1.0 / nfeat, bias=epsb),
                     reads=[psb[b]], writes=[Brd])
                P.op(DVE, lambda e: e.reciprocal(out=rdst, in_=rdst), reads=[Brd], writes=[Brd])
                return
            P.op(ACT, lambda e: e.activation(out=rdst, in_=bank(b), func=AF.Ln, scale=1.0 / nfeat, bias=epsb),
                 reads=[psb[b]], writes=[Brd])
            P.op(ACT, lambda e: e.activation(out=rdst, in_=rdst, func=AF.Exp, scale=-0.5), reads=[Brd], writes=[Brd])

        def proj_fm(W, ncs, col0, rhs, Brhs, ncols=128, BWx=None):
            BWx = BWx or BW
            b = next_bank()
            for c in range(ncs):
                P.op(PE, lambda e, c=c: e.matmul(bank(b)[0:ncols, :], lhsT=W[:, c, col0:col0 + ncols], rhs=rhs[:, c, :],
                                                 start=(c == 0), stop=(c == ncs - 1)),
                     reads=[BWx, Brhs], writes=[psb[b]] if c in (0, ncs - 1) else [], signal=(c == ncs - 1))
            return b

        rawctr = [0]

        def rope_a(W, ncs, col0, rhs, Brhs):
            bA = proj_fm(W, ncs, col0, rhs, Brhs)
            k = rawctr[0] % 2
            rawctr[0] += 1
            P.op(ACT, lambda e: e.activation(out=rawbf[k], in_=bank(bA), func=AF.Copy), reads=[psb[bA]], writes=[Braw[k]])
            return (bA, k)

        def rope_b(st, dst, Bdst, nb):
            bA, k = st
            bB = next_bank()
            P.op(PE, lambda e: e.matmul(bank(bB), lhsT=permb, rhs=rawbf[k], start=True, stop=True), reads=[Braw[k], BW], writes=[psb[bB]])
            mA, mB, BmA, BmB = mA2[k], mB2[k], BmA2[k], BmB2[k]
            P.op(DVE, lambda e: e.tensor_tensor(out=mA, in0=bank(bA), in1=cos2[nb], op=ALU.mult), reads=[psb[bA], Btab2[nb], Braw[k]], writes=[BmA])
            P.op(DVE, lambda e: e.tensor_tensor(out=mB, in0=bank(bB), in1=sin2[nb], op=ALU.mult), reads=[psb[bB], Btab2[nb]], writes=[BmB])
            P.op(POOL, lambda e: e.tensor_tensor(out=dst, in0=mA, in1=mB, op=ALU.add), reads=[BmA, BmB], writes=[Bdst])

        def rope_tiles(specs, nb):
            prev = None
            for sp_ in specs:
                st = rope_a(*sp_[0:5])
                if prev is not None:
                    rope_b(prev[0], prev[1], prev[2], nb)
                prev = (st, sp_[5], sp_[6])
            rope_b(prev[0], prev[1], prev[2], nb)

        groups = [(slot, G) for slot in range(NSLOT) for G in range(4)]
        if os.environ.get("NG"):
            groups = groups[:int(os.environ["NG"])]

        def load_x(n):
            slot, G = groups[n]
            P.dma(SP, xTg.rearrange("p c t -> p (c t)"), xT[slot * 4 + G], writes=[BxT])

        def norm(n, part=0):
            slot, G = groups[n]
            nb = n % 2
            full = slot == 0
            hTg = hTg2[nb]
            tg = slice(G * 512, (G + 1) * 512)
            if part in (0, 1):
                P.op(ACT, lambda e: e.activation(out=sqg, in_=xTg, func=AF.Square), reads=[BxT], writes=[Bsq])
                if part == 1:
                    return
            if part == 3:
                return norm_tables(n)
            b = next_bank()
            for c in range(8):
                P.op(PE, lambda e, c=c, b=b: e.matmul(bank(b), lhsT=ones, rhs=sqg[:, c, :], start=(c == 0), stop=(c == 7)),
                     reads=[Bsq], writes=[psb[b]] if c in (0, 7) else [], signal=(c == 7))
            rstd_from_psum(b, 1024.0, rstd2[nb], Brstd2[nb])
            for c in range(8):
                P.op(DVE, lambda e, c=c: e.scalar_tensor_tensor(out=hTg[:, c, :], in0=xTg[:, c, :], scalar=g_in[:, c:c + 1],
                                                                 in1=rstd2[nb], op0=ALU.mult, op1=ALU.mult),
                     reads=[BxT, Brstd2[nb]], writes=[BhT2[nb]])
            if n + 1 < len(groups):
                load_x(n + 1)
            if full:
                P.dma(SP, hT_d.rearrange("p (c t) -> p c t", c=8)[:, :, tg], hTg, reads=[BhT2[nb]], writes=[BhTd], key="hTd%d" % nb)
            if part == 2:
                return
            norm_tables(n)

        def load_pos(n):
            slot, G = groups[n]
            P.dma(SP, posi, pbcast(pos[:, slot * NT + G * 512:slot * NT + (G + 1) * 512]), writes=[Bpos], key="tab")

        def norm_tables(n):
            slot, G = groups[n]
            nb = n % 2
            P.op(DVE, lambda e: e.tensor_copy(out=tA, in_=posi), reads=[Bpos], writes=[BtA])
            P.op(DVE, lambda e: e.tensor_scalar(out=tA, in0=tA, scalar1=invf, scalar2=None, op0=ALU.mult), reads=[BtA], writes=[BtA])
            ki = posi
            P.op(DVE, lambda e: e.tensor_scalar(out=ki, in0=tA, scalar1=float(1.0 / (2 * np.pi)), scalar2=None, op0=ALU.mult),
                 reads=[BtA], writes=[Bpos])
            P.op(DVE, lambda e: e.tensor_copy(out=tB, in_=ki), reads=[Bpos], writes=[BtB])
            C1 = 6.28125
            C2 = float(np.float32(2 * np.pi - 6.28125))
            P.op(DVE, lambda e: e.scalar_tensor_tensor(out=tA, in0=tB, scalar=-C1, in1=tA, op0=ALU.mult, op1=ALU.add), reads=[BtB, BtA], writes=[BtA])
            P.op(DVE, lambda e: e.scalar_tensor_tensor(out=tA, in0=tB, scalar=-C2, in1=tA, op0=ALU.mult, op1=ALU.add), reads=[BtB, BtA], writes=[BtA])
            PI_LO = 3.1415925
            P.op(DVE, lambda e: e.tensor_scalar(out=tA, in0=tA, scalar1=PI_LO, scalar2=-PI_LO, op0=ALU.min, op1=ALU.max), reads=[BtA], writes=[BtA])
            P.op(ACT, lambda e: e.activation(out=sin2[nb], in_=tA, func=AF.Sin, scale=sgn), reads=[BtA], writes=[Btab2[nb]])
            P.op(DVE, lambda e: e.tensor_scalar(out=tB, in0=tA, scalar1=float(np.pi / 2), scalar2=float(np.pi), op0=ALU.add, op1=ALU.is_gt),
                 reads=[BtA], writes=[BtB])
            P.op(DVE, lambda e: e.scalar_tensor_tensor(out=tB, in0=tB, scalar=-float(2 * np.pi), in1=tA, op0=ALU.mult, op1=ALU.add),
                 reads=[BtB, BtA], writes=[BtB])
            P.op(DVE, lambda e: e.tensor_scalar(out=tB, in0=tB, scalar1=float(np.pi / 2), scalar2=PI_LO, op0=ALU.add, op1=ALU.min), reads=[BtB], writes=[BtB])
            P.op(DVE, lambda e: e.tensor_scalar(out=tB, in0=tB, scalar1=-PI_LO, scalar2=None, op0=ALU.max), reads=[BtB], writes=[BtB])
            P.op(ACT, lambda e: e.activation(out=cos2[nb], in_=tB, func=AF.Sin), reads=[BtB], writes=[Btab2[nb]])

        def proj(n):
            slot, G = groups[n]
            nb = n % 2
            full = slot == 0
            hTg, BhT = hTg2[nb], BhT2[nb]
            tg = slice(G * 512, (G + 1) * 512)
            kvdst = kv_in if USE_CC else kv_all[slot * 128:(slot + 1) * 128, :]
            more = n + 1 < len(groups)
            if more:
                load_pos(n + 1)

            def vdiff(tts):
                for tt in tts:
                    b = next_bank()
                    for c in range(8):
                        P.op(PE, lambda e, c=c, b=b, tt=tt: e.matmul(bank(b), lhsT=hTg[:, c, tt * 128:(tt + 1) * 128], rhs=Wdv[:, c, :],
                                                                     start=(c == 0), stop=(c == 7)),
                             reads=[BW, BhT], writes=[psb[b]] if c in (0, 7) else [], signal=(c == 7))
                    P.op(ACT, lambda e, b=b, tt=tt: e.activation(out=Vd[:, tt, :, 0:128], in_=bank(b).rearrange("p (h j) -> p h j", h=4), func=AF.Copy),
                         reads=[psb[b]], writes=[BVd])

            for j in range(2):
                b = proj_fm(Wfm, 8, (12 + j) * 128, hTg, BhT)
                P.op(ACT, lambda e, j=j, b=b: e.activation(out=ckvraw[:, j, :], in_=bank(b), func=AF.Copy), reads=[psb[b]], writes=[Bckv])
                P.op(ACT, lambda e, j=j, b=b: e.activation(out=sq2[:, 3 + j, :], in_=bank(b), func=AF.Square), reads=[psb[b]], writes=[Bsq2])
            if full:
                for j in range(3):
                    b = proj_fm(Wfm, 8, (9 + j) * 128, hTg, BhT)
                    P.op(ACT, lambda e, j=j, b=b: e.activation(out=cqraw[:, j, :], in_=bank(b), func=AF.Copy), reads=[psb[b]], writes=[Bcq])
                    P.op(ACT, lambda e, j=j, b=b: e.activation(out=sq2[:, j, :], in_=bank(b), func=AF.Square), reads=[psb[b]], writes=[Bsq2])
            specs = [(Wfm, 8, (4 + h) * 128, hTg, BhT, kTd[:, h, :], BkTd) for h in range(4)]
            specs.append((Wfm, 8, 8 * 128, hTg, BhT, kTr, BkTr))
            rope_tiles(specs[0:2], nb)
            if more:
                norm(n + 1, part=1)
            b = next_bank()
            for j in range(2):
                P.op(PE, lambda e, j=j, b=b: e.matmul(bank(b), lhsT=ones, rhs=sq2[:, 3 + j, :], start=(j == 0), stop=(j == 1)),
                     reads=[Bsq2], writes=[psb[b]] if j in (0, 1) else [], signal=(j == 1))
            rstd_from_psum(b, 256.0, rstdkv, Brkv)
            for j in range(2):
                P.op(DVE, lambda e, j=j: e.scalar_tensor_tensor(out=ckvn[:, j, :], in0=ckvraw[:, j, :], scalar=g_kv[:, j:j + 1], in1=rstdkv,
                                                                op0=ALU.mult, op1=ALU.mult), reads=[Bckv, Brkv], writes=[Bckvn])
            if full:
                b = next_bank()
                for j in range(3):
                    P.op(PE, lambda e, j=j, b=b: e.matmul(bank(b), lhsT=ones, rhs=sq2[:, j, :], start=(j == 0), stop=(j == 2)),
                         reads=[Bsq2], writes=[psb[b]] if j in (0, 2) else [], signal=(j == 2))
                rstd_from_psum(b, 384.0, rstdq, Brq)
                for j in range(3):
                    P.op(DVE, lambda e, j=j: e.scalar_tensor_tensor(out=cqn[:, j, :], in0=cqraw[:, j, :], scalar=g_q[:, j:j + 1], in1=rstdq,
                                                                    op0=ALU.mult, op1=ALU.mult), reads=[Bcq, Brq], writes=[Bcqn])
            rope_tiles(specs[2:5], nb)
            P.dma(SP, kvin_k(kvdst, KD)[:, :, tg], kTd, reads=[BkTd], writes=[Bkvin], key="kvin_kTd")
            P.dma(SP, kvdst[:, KR + G * 512:KR + (G + 1) * 512], kTr, reads=[BkTr], writes=[Bkvin], key="kvin_kTr")
            if more:
                norm(n + 1, part=2)
            vdiff([0, 1])
            if full:
                rope_tiles([(Wfm, 8, h * 128, hTg, BhT, QTd[:, h, tg], BQ) for h in range(4)], nb)
            vdiff([2, 3])
            for tt in range(4):
                P.dma(SP, kvin_v(kvdst, VD)[:, 4 * G + tt, :, :], Vd[:, tt, :, :], reads=[BVd], writes=[Bkvin], key="kvin_Vd")
            if full:
                for h in range(4):
                    b = proj_fm(Wuq, 3, h * 128, cqn, Bcqn)
                    P.op(ACT, lambda e, h=h, b=b, tg=tg: e.activation(out=QTn[:, h, tg], in_=bank(b), func=AF.Copy), reads=[psb[b]], writes=[BQ])
                rope_tiles([(Wuq, 3, (4 + pr) * 128, cqn, Bcqn, QTr[:, pr, tg], BQ) for pr in range(2)], nb)
            for h in range(4):
                b = proj_fm(Wukv, 2, h * 128, ckvn, Bckvn)
                P.op(ACT, lambda e, h=h, b=b: e.activation(out=kTn[:, h, :], in_=bank(b), func=AF.Copy), reads=[psb[b]], writes=[BkTn])
            P.dma(SP, kvin_k(kvdst, KN)[:, :, tg], kTn, reads=[BkTn], writes=[Bkvin], key="kvin_kTn")
            for tt in range(4):
                b = next_bank()
                for j in range(2):
                    P.op(PE, lambda e, j=j, b=b, tt=tt: e.matmul(bank(b), lhsT=ckvn[:, j, tt * 128:(tt + 1) * 128], rhs=Wukv[:, j, 512:1024],
                                                                 start=(j == 0), stop=(j == 1)),
                         reads=[BW, Bckvn], writes=[psb[b]] if j in (0, 1) else [], signal=(j == 1))
                P.op(DVE, lambda e, b=b, tt=tt: e.tensor_copy(out=Vm[:, tt, :, 0:128], in_=bank(b).rearrange("p (h j) -> p h j", h=4)),
                     reads=[psb[b]], writes=[BVm])
            for tt in range(4):
                P.dma(SP, kvin_v(kvdst, VM)[:, 4 * G + tt, :, :], Vm[:, tt, :, :], reads=[BVm], writes=[Bkvin], key="kvin_Vm")
            if more:
                norm(n + 1, part=3)

        load_x(0)
        load_pos(0)
        norm(0)
        for n in range(len(groups)):
            proj(n)

        P.barrier()
        if stop == 1:
            P.dma(SP, dbg["kv"], kv_in if USE_CC else kv_all[0:128, :], writes=[Buf("dbgkv")], key="dbg")
            P.dma(SP, dbg["qt"][:, 0:4 * NT], QTd.rearrange("p h t -> p (h t)"), reads=[BQ], writes=[Buf("dbgq")], key="dbg")
            P.dma(SP, dbg["qt"][:, 4 * NT:8 * NT], QTn.rearrange("p h t -> p (h t)"), reads=[BQ], writes=[Buf("dbgq")], key="dbg")
            P.dma(SP, dbg["qt"][:, 8 * NT:10 * NT], QTr.rearrange("p h t -> p (h t)"), reads=[BQ], writes=[Buf("dbgq")], key="dbg")
            P.barrier()
            P.emit()
            return nc
        if USE_CC:
            P.raw(POOL, lambda e: e.collective_compute("AllGather", ALU.bypass, replica_groups=[list(range(NCORES))],
                                                       ins=[kv_in], outs=[kv_all]).then_inc(cc_sem, 1))
            for eng in ENGS:
                P.raw(eng, lambda e: e.wait_ge(cc_sem, 1))
        XP = os.environ.get("XP", "")
        if "a" in XP:
            P.barrier()
        if "b" in XP:
            scr = nc.dram_tensor("scr", [128, KVCOLS], BF16, kind="Internal").ap()
            P.dma(SP, scr, kv_in, writes=[Buf("scr")], key="scr")
            P.barrier()
        if debug and "B" in os.environ.get("DBG", "BC"):
            P.dma(SP, dbg["kv"], kv_in if USE_CC else kv_all[0:128, :], writes=[Buf("dbgkv")], key="dbg")
            P.dma(SP, dbg["qt"][:, 0:4 * NT], QTd.rearrange("p h t -> p (h t)"), reads=[BQ], writes=[Buf("dbgq")], key="dbg")
            P.dma(SP, dbg["qt"][:, 4 * NT:8 * NT], QTn.rearrange("p h t -> p (h t)"), reads=[BQ], writes=[Buf("dbgq")], key="dbg")
            P.dma(SP, dbg["qt"][:, 8 * NT:10 * NT], QTr.rearrange("p h t -> p (h t)"), reads=[BQ], writes=[Buf("dbgq")], key="dbg")
            P.barrier()

        A.cur = X0
        oT = A.alloc(8 * NT, BF16).rearrange("p (h t) -> p h t", h=8)
        BoT = Buf("oT")
        X1 = A.cur
        NKB = 3
        KTc = [A.alloc(8 * 512, BF16).rearrange("p (r n) -> p r n", r=8) for _ in range(NKB)]
        KRc = [A.alloc(8 * 512, BF16).rearrange("p (r n) -> p r n", r=8) for _ in range(NKB)]
        Vc = [A.alloc(8 * 4 * 129, BF16).rearrange("p (r n) -> p r n", r=8) for _ in range(NKB)]
        BKV = [Buf("kvc%d" % i) for i in range(NKB)]
        for i in range(NKB):
            P.op(POOL, lambda e, i=i: e.memset(KTc[i][64:128, :, :], 0.0), writes=[BKV[i]])
            P.op(POOL, lambda e, i=i: e.memset(KRc[i][0:64, :, :], 0.0), writes=[BKV[i]])
        NPT = 4
        PT = [A.alloc(1024, BF16) for _ in range(NPT)]
        BPT = [Buf("pt%d" % i) for i in range(NPT)]
        NPS = 2
        PSm = [A.alloc(1024, BF16) for _ in range(NPS)]
        BPSm = [Buf("psm%d" % i) for i in range(NPS)]
        Osb = A.alloc(4 * 512, F32)
        BOsb = Buf("Osb")
        ofin = A.alloc(512, F32)
        osq = A.alloc(512, BF16)
        rsb = A.alloc(512, F32)
        cm1 = A.alloc(1024, F32)
        cmh = A.alloc(512, F32)
        Bofin, Bosq, Brsb = Buf("ofin"), Buf("osq"), Buf("rsb")
        P.op(POOL, lambda e: e.memset(cm1, -1.0), writes=[Bc])
        P.op(POOL, lambda e: e.memset(cmh, -0.5), writes=[Bc])
        BO = Buf("Oacc")
        BS = [Buf("S0"), Buf("S1")]
        kvall_v = kv_all.rearrange("(r p) n -> p r n", p=128)
        NS = 8

        def Sslot(s):
            return ps[:, 2048 + s * 1024:2048 + (s + 1) * 1024]

        jobs = []
        for h in range(4):
            for u in range(4):
                for ci in range(u + 1):
                    jobs.append(("d", h, u, ci))
        for h in range(4):
            for p_ in range(2):
                for ci in range(2 * p_ + 2):
                    jobs.append(("m", h, p_, ci))

        def load_chunk(ji):
            kind, h, _, ci = jobs[ji]
            s = ji % NKB
            g0 = 4 * ci
            if kind == "d":
                P.dma(SP, KTc[s][0:64, :, :], kvall_v[0:64, :, KD + h * NT + g0 * 128:KD + h * NT + g0 * 128 + 512], writes=[BKV[s]])
                P.dma(SP, KRc[s][64:128, :, :], kvall_v[64:128, :, KD + h * NT + g0 * 128:KD + h * NT + g0 * 128 + 512], writes=[BKV[s]])
                P.dma(SP, Vc[s], kvall_v[:, :, VD + (h * TPC + g0) * 129:VD + (h * TPC + g0 + 4) * 129], writes=[BKV[s]])
            else:
                P.dma(SP, KTc[s], kvall_v[:, :, KN + h * NT + g0 * 128:KN + h * NT + g0 * 128 + 512], writes=[BKV[s]])
                lo = (h % 2) * 64
                zl = 64 - lo
                P.op(POOL, lambda e, s=s, zl=zl: e.memset(KRc[s][zl:zl + 64, :, :], 0.0), writes=[BKV[s]])
                P.dma(SP, KRc[s][lo:lo + 64, :, :], kvall_v[lo:lo + 64, :, KR + g0 * 128:KR + g0 * 128 + 512], writes=[BKV[s]])
                P.dma(SP, Vc[s], kvall_v[:, :, VM + (h * TPC + g0) * 129:VM + (h * TPC + g0 + 4) * 129], writes=[BKV[s]])

        steps = []
        for ji, (kind, h, qp, ci) in enumerate(jobs):
            nq = 4 if kind == "d" else 8
            mbase = nq * qp
            for gg in range(4):
                g = 4 * ci + gg
                if g > mbase + nq - 1:
                    continue
                for r in range(8):
                    steps.append((ji, kind, h, qp, g, gg, r))
        last_step_of_pass = {}
        for t, st in enumerate(steps):
            last_step_of_pass[(st[1], st[2], st[3])] = t
        nsteps = len(steps)
        slot_of = {}
        sctr = [0]

        def take_slot():
            s = sctr[0] % 2
            sctr[0] += 1
            return s

        SCALE_D = 64 ** -0.5
        SCALE_M = 192 ** -0.5

        def geom(t):
            ji, kind, h, qp, g, gg, r = steps[t]
            nq = 4 if kind == "d" else 8
            mbase = nq * qp
            m_lo = max(mbase, g)
            cols = (mbase + nq - m_lo) * 128
            c_lo = (m_lo - mbase) * 128
            return nq, mbase, m_lo, cols, c_lo

        def issue_qk(t):
            ji, kind, h, qp, g, gg, r = steps[t]
            nq, mbase, m_lo, cols, c_lo = geom(t)
            s = ji % NKB
            sl = take_slot()
            slot_of[t] = sl
            diag = g >= mbase
            q0 = m_lo * 128
            Sv = Sslot(sl)
            kcols = slice(gg * 128, (gg + 1) * 128)
            segs = []
            c0 = 0
            if kind == "d":
                segs.append((0, cols, diag))
            else:
                while c0 < cols:
                    c1 = min(cols, (c0 // 512 + 1) * 512)
                    segs.append((c0, c1, diag and c0 == 0))
                    c0 = c1
            mm = []
            if kind == "d":
                for mp in range(2):
                    kbuf = KTc[s] if mp == 0 else KRc[s]
                    for (a, b_, dg) in segs:
                        mm.append((Sv[:, mp * 512 + a:mp * 512 + b_], kbuf[:, r, kcols], QTd[:, h, q0 + a:q0 + b_], True, not dg))
                        if dg:
                            mm.append((Sv[:, mp * 512:mp * 512 + 128], ident, maskb[:, r, :], False, True))
            else:
                rows = slice((h % 2) * 64, (h % 2) * 64 + 64)
                for (a, b_, dg) in segs:
                    mm.append((Sv[:, a:b_], KTc[s][:, r, kcols], QTn[:, h, q0 + a:q0 + b_], True, False))
                    mm.append((Sv[:, a:b_], KRc[s][:, r, kcols], QTr[:, h // 2, q0 + a:q0 + b_], False, not dg))
                    if dg:
                        mm.append((Sv[:, 0:128], ident, maskb[:, r, :], False, True))
            for i, (o_, l_, r_, st_, sp_) in enumerate(mm):
                last = i == len(mm) - 1
                P.op(PE, lambda e, o_=o_, l_=l_, r_=r_, st_=st_, sp_=sp_: e.matmul(o_, lhsT=l_, rhs=r_, start=st_, stop=sp_, skip_group_check=True),
                     reads=[BKV[s], BQ], writes=[BS[sl]] if (i == 0 or last) else [], signal=last)

        def pt_view(buf, kind, cols):
            if kind == "d":
                return buf.rearrange("p (a n) -> p a n", a=2)[:, :, 0:cols]
            return buf[:, 0:cols]

        def issue_exp(t):
            ji, kind, h, qp, g, gg, r = steps[t]
            nq, mbase, m_lo, cols, c_lo = geom(t)
            sl = slot_of[t]
            pt = t % NPT
            src = pt_view(Sslot(sl), kind, cols)
            dst = pt_view(PT[pt], kind, cols)
            sc = SCALE_D if kind == "d" else SCALE_M
            P.op(ACT, lambda e: e.activation(out=dst, in_=src, func=AF.Exp, scale=sc), reads=[BS[sl]], writes=[BPT[pt]])

        def out_segs(kind, c_lo, cols):
            if kind == "d":
                return [(mp, c_lo, c_lo + cols, mp * 512) for mp in range(2)]
            res = []
            a = c_lo
            while a < c_lo + cols:
                b_ = min(c_lo + cols, (a // 512 + 1) * 512)
                res.append((a // 512, a % 512, a % 512 + (b_ - a), a - c_lo))
                a = b_
            return res

        def issue_pv(t):
            ji, kind, h, qp, g, gg, r = steps[t]
            nq, mbase, m_lo, cols, c_lo = geom(t)
            s = ji % NKB
            pt = t % NPT
            first = (g == 0 and r == 0)
            vt = Vc[s][:, r, gg * 129:gg * 129 + 128]
            pieces = out_segs(kind, c_lo, cols)
            for i, (bk, a, b_, po) in enumerate(pieces):
                o_ = ps[:, bk * 512 + a:bk * 512 + b_]
                r_ = PT[pt][:, po:po + (b_ - a)]
                P.op(PE, lambda e, o_=o_, r_=r_, first=first: e.matmul(o_, lhsT=vt, rhs=r_, start=first, stop=False, skip_group_check=True),
                     reads=[BPT[pt], BKV[s]], writes=[BO] if i == 0 else [], signal=False)

        def issue_preadd(t):
            ji, kind, h, qp, g, gg, r = steps[t]
            nq, mbase, m_lo, cols, c_lo = geom(t)
            k = r % NS
            if k == 0:
                return
            pi = (t // NS) % NPS
            dst = pt_view(PSm[pi], kind, cols)
            a_ = pt_view(PT[(t - 1) % NPT], kind, cols) if k == 1 else dst
            b_ = pt_view(PT[t % NPT], kind, cols)
            rd = [BPT[t % NPT]] + ([BPT[(t - 1) % NPT]] if k == 1 else [BPSm[pi]])
            P.op(DVE, lambda e: e.tensor_tensor(out=dst, in0=a_, in1=b_, op=ALU.add), reads=rd, writes=[BPSm[pi]])

        def issue_den(t):
            ji, kind, h, qp, g, gg, r = steps[t]
            if r % NS != NS - 1:
                return
            nq, mbase, m_lo, cols, c_lo = geom(t)
            pi = (t // NS) % NPS
            first = (g == 0 and r == NS - 1)
            is_last = last_step_of_pass[(kind, h, qp)] == t
            pieces = out_segs(kind, c_lo, cols)
            for i, (bk, a, b_, po) in enumerate(pieces):
                o_ = ps[:, (2 + bk) * 512 + a:(2 + bk) * 512 + b_]
                r_ = PSm[pi][:, po:po + (b_ - a)]
                last = is_last and i == len(pieces) - 1
                P.op(PE, lambda e, o_=o_, r_=r_, first=first: e.matmul(o_, lhsT=ones, rhs=r_, start=first, stop=False, skip_group_check=True),
                     reads=[BPSm[pi]], writes=[BO] if (i == 0 or last) else [], signal=last)

        deferred = []

        def finish_pass(kind, h, qp, t):
            nq = 4 if kind == "d" else 8
            q0 = nq * qp * 128
            P.op(DVE, lambda e: e.tensor_copy(out=Osb[:, 0:1024], in_=ps[:, 0:1024]), reads=[BO], writes=[BOsb])
            P.op(ACT, lambda e: e.activation(out=Osb[:, 1024:2048], in_=ps[:, 1024:2048], func=AF.Ln), reads=[BO], writes=[BOsb])
            P.op(ACT, lambda e: e.activation(out=Osb[:, 1024:2048], in_=Osb[:, 1024:2048], func=AF.Exp, scale=-1.0), reads=[BOsb], writes=[BOsb])
            if kind == "m":
                P.op(POOL, lambda e: e.tensor_tensor(out=oT[:, 4 + h, q0:q0 + 1024], in0=Osb[:, 0:1024], in1=Osb[:, 1024:2048], op=ALU.mult),
                     reads=[BOsb], writes=[BoT])
                return
            P.op(POOL, lambda e: e.tensor_tensor(out=Osb[:, 0:1024], in0=Osb[:, 0:1024], in1=Osb[:, 1024:2048], op=ALU.mult), reads=[BOsb], writes=[BOsb])
            P.op(POOL, lambda e: e.tensor_scalar(out=Osb[:, 512:1024], in0=Osb[:, 512:1024], scalar1=neglam, scalar2=None, op0=ALU.mult),
                 reads=[BOsb], writes=[BOsb])
            P.op(POOL, lambda e: e.tensor_tensor(out=ofin, in0=Osb[:, 0:512], in1=Osb[:, 512:1024], op=ALU.add), reads=[BOsb], writes=[Bofin])
            P.op(POOL, lambda e: e.tensor_tensor(out=osq, in0=ofin, in1=ofin, op=ALU.mult), reads=[Bofin], writes=[Bosq])

            def part_b(h=h, q0=q0):
                sl = take_slot()
                P.op(PE, lambda e: e.matmul(Sslot(sl)[:, 0:512], lhsT=ones, rhs=osq, start=True, stop=True), reads=[Bosq], writes=[BS[sl]])
                P.op(ACT, lambda e: e.activation(out=rsb, in_=Sslot(sl)[:, 0:512], func=AF.Ln, scale=1.0 / 128, bias=epsb),
                     reads=[BS[sl]], writes=[Brsb])
                P.op(ACT, lambda e: e.activation(out=rsb, in_=rsb, func=AF.Exp, scale=-0.5), reads=[Brsb], writes=[Brsb])
                P.op(POOL, lambda e: e.tensor_scalar(out=ofin, in0=ofin, scalar1=sublnc, scalar2=None, op0=ALU.mult), reads=[Bofin], writes=[Bofin])
                P.op(POOL, lambda e: e.tensor_tensor(out=oT[:, h, q0:q0 + 512], in0=ofin, in1=rsb, op=ALU.mult), reads=[Bofin, Brsb], writes=[BoT])
            deferred.append((t + 16, part_b))

        loaded = -1

        def ensure_loaded(upto):
            nonlocal loaded
            while loaded < min(upto, len(jobs) - 1):
                loaded += 1
                load_chunk(loaded)

        ensure_loaded(1)
        issue_qk(0)
        issue_exp(0)
        for t in range(nsteps + 1):
            if t + 1 < nsteps:
                issue_qk(t + 1)
                issue_exp(t + 1)
            u = t - 1
            if u < 0:
                continue
            ensure_loaded(steps[u][0] + NKB - 1)
            prev_last = u > 0 and last_step_of_pass[(steps[u - 1][1], steps[u - 1][2], steps[u - 1][3])] == u - 1
            if prev_last:
                issue_den(u - 1)
                finish_pass(steps[u - 1][1], steps[u - 1][2], steps[u - 1][3], u - 1)
            issue_pv(u)
            issue_preadd(u)
            if u > 0 and not prev_last:
                issue_den(u - 1)
            while deferred and deferred[0][0] <= u:
                deferred.pop(0)[1]()
        issue_den(nsteps - 1)
        finish_pass(steps[-1][1], steps[-1][2], steps[-1][3], nsteps - 1)
        while deferred:
            deferred.pop(0)[1]()

        P.barrier()
        if debug and "C" in os.environ.get("DBG", "BC"):
            P.dma(SP, dbg["ot"], oT.rearrange("p h t -> p (h t)"), reads=[BoT], writes=[Buf("dbgo")], key="dbg")
            P.barrier()
        if stop == 2:
            P.emit()
            return nc

        A.cur = X1
        Wg3 = A.alloc(8 * 3072, BF16).rearrange("p (c n) -> p c n", c=8)
        Wpd = A.alloc(4 * DM, BF16).rearrange("p (c n) -> p c n", c=4)
        Wpm = A.alloc(4 * DM, BF16).rearrange("p (c n) -> p c n", c=4)
        Wout = A.alloc(8 * DM, BF16).rearrange("p (c n) -> p c n", c=8)
        hT3b = [A.alloc(8 * 512, BF16).rearrange("p (c n) -> p c n", c=8) for _ in range(2)]
        oTg = A.alloc(8 * 512, BF16).rearrange("p (c n) -> p c n", c=8)
        mT = A.alloc(8 * 512, BF16).rearrange("p (c n) -> p c n", c=8)
        P3B_END = A.cur
        A.cur = CONST_END
        sg = [A.alloc(512, F32) for _ in range(2)]
        sgd = [A.alloc(512, F32) for _ in range(2)]
        sgm = [A.alloc(512, F32) for _ in range(2)]
        m1 = A.alloc(512, F32)
        m2 = A.alloc(512, F32)
        xt = [A.alloc(DM, F32) for _ in range(2)]
        yb = [A.alloc(DM, F32) for _ in range(2)]
        junk = A.alloc(DM, F32)
        sm3 = A.alloc(16, F32)
        assert A.cur <= X0, (A.cur, X0)
        BWs, BWp, BWg, BWo = Buf("w3s"), Buf("w3p"), Buf("w3g"), Buf("w3o")
        wg3_v = wg3.rearrange("(c p) n -> p c n", p=128)
        for c in range(8):
            P.dma(POOL, Wg3[:, c, 0:1024], wg3_v[:, c, 0:1024], writes=[BWs], key="w3s")
        P.dma(POOL, Wpd, wpd.rearrange("(c p) n -> p c n", p=128), writes=[BWp], key="w3p")
        P.dma(POOL, Wpm, wpm.rearrange("(c p) n -> p c n", p=128), writes=[BWp], key="w3p")
        for c in range(8):
            P.dma(POOL, Wg3[:, c, 1024:3072], wg3_v[:, c, 1024:3072], writes=[BWg], key="w3g")
        wout_v = wout.rearrange("(c p) n -> p c n", p=128)
        for c in range(0, 8, 2):
            P.dma(POOL, Wout[:, c:c + 2, :], wout_v[:, c:c + 2, :], writes=[BWo], key="w3o")
        BhT3b, BoTg, BmT = [Buf("hT3a"), Buf("hT3b")], Buf("oTg"), Buf("mT")
        Bsg = [Buf("sg0"), Buf("sg1")]
        Bsgd = [Buf("sgd0"), Buf("sgd1")]
        Bsgm = [Buf("sgm0"), Buf("sgm1")]
        Bm1, Bm2 = Buf("m1"), Buf("m2")
        Bxt = [Buf("xt0"), Buf("xt1")]
        Byb = [Buf("yb0"), Buf("yb1")]
        Bjunk, Bsm3 = Buf("junk"), Buf("sm3")
        By = Buf("y")
        hTd_v = hT_d.rearrange("p (c t) -> p c t", c=8)
        ti = 0
        P.dma(SP, hT3b[0], hTd_v[:, :, 0:512], reads=[BhTd], writes=[BhT3b[0]])
        for G in range(4):
            tg = slice(G * 512, (G + 1) * 512)
            hT3, BhT3 = hT3b[G % 2], BhT3b[G % 2]
            if G + 1 < 4:
                P.dma(SP, hT3b[(G + 1) % 2], hTd_v[:, :, (G + 1) * 512:(G + 2) * 512], reads=[BhTd], writes=[BhT3b[(G + 1) % 2]])
            for j in range(8):
                b = proj_fm(Wg3, 8, j * 128, hT3, BhT3, BWx=BWs)
                k = j % 2
                P.op(ACT, lambda e, b=b, k=k: e.activation(out=sg[k], in_=bank(b), func=AF.Silu), reads=[psb[b]], writes=[Bsg[k]])
                P.op(DVE, lambda e, j=j, k=k, tg=tg: e.tensor_tensor(out=oTg[:, j, :], in0=oT[:, j, tg], in1=sg[k], op=ALU.mult),
                     reads=[Bsg[k], BoT], writes=[BoTg])
            for n in range(8):
                k = n % 2
                bd = next_bank()
                for hh in range(4):
                    P.op(PE, lambda e, hh=hh, bd=bd, n=n: e.matmul(bank(bd), lhsT=Wpd[:, hh, n * 128:(n + 1) * 128], rhs=oTg[:, hh, :],
                                                                  start=(hh == 0), stop=(hh == 3)),
                         reads=[BWp, BoTg], writes=[psb[bd]] if hh in (0, 3) else [], signal=(hh == 3))
                bm = next_bank()
                for hh in range(4):
                    P.op(PE, lambda e, hh=hh, bm=bm, n=n: e.matmul(bank(bm), lhsT=Wpm[:, hh, n * 128:(n + 1) * 128], rhs=oTg[:, 4 + hh, :],
                                                                  start=(hh == 0), stop=(hh == 3)),
                         reads=[BWp, BoTg], writes=[psb[bm]] if hh in (0, 3) else [], signal=(hh == 3))
                bgd = proj_fm(Wg3, 8, 1024 + n * 128, hT3, BhT3, BWx=BWg)
                bgm = proj_fm(Wg3, 8, 2048 + n * 128, hT3, BhT3, BWx=BWg)
                P.op(ACT, lambda e, b=bgd, k=k: e.activation(out=sgd[k], in_=bank(b), func=AF.Sigmoid), reads=[psb[bgd]], writes=[Bsgd[k]])
                P.op(ACT, lambda e, b=bgm, k=k: e.activation(out=sgm[k], in_=bank(b), func=AF.Sigmoid), reads=[psb[bgm]], writes=[Bsgm[k]])
                P.op(DVE, lambda e, bd=bd, k=k: e.tensor_tensor(out=m1, in0=bank(bd), in1=sgd[k], op=ALU.mult), reads=[psb[bd], Bsgd[k]], writes=[Bm1])
                P.op(DVE, lambda e, bm=bm, k=k: e.tensor_tensor(out=m2, in0=bank(bm), in1=sgm[k], op=ALU.mult), reads=[psb[bm], Bsgm[k]], writes=[Bm2])
                P.op(POOL, lambda e, n=n: e.tensor_tensor(out=mT[:, n, :], in0=m1, in1=m2, op=ALU.add), reads=[Bm1, Bm2], writes=[BmT])
            for tt in range(4):
                k = ti % 2
                ti += 1
                row0 = G * 512 + tt * 128
                P.dma(SP, xt[k], xtok[row0:row0 + 128, :], writes=[Bxt[k]])
                for hf in range(2):
                    b = next_bank()
                    for n in range(8):
                        P.op(PE, lambda e, n=n, b=b, tt=tt, hf=hf: e.matmul(bank(b), lhsT=mT[:, n, tt * 128:(tt + 1) * 128], rhs=Wout[:, n, hf * 512:(hf + 1) * 512],
                                                                       start=(n == 0), stop=(n == 7)),
                             reads=[BWo, BmT], writes=[psb[b]] if n in (0, 7) else [], signal=(n == 7))
                    P.op(DVE, lambda e, b=b, k=k, hf=hf: e.tensor_tensor(out=yb[k][:, hf * 512:(hf + 1) * 512], in0=bank(b), in1=xt[k][:, hf * 512:(hf + 1) * 512], op=ALU.add),
                         reads=[psb[b], Bxt[k]], writes=[Byb[k]])
                P.op(DVE, lambda e, k=k: e.scalar_tensor_tensor(out=junk, in0=yb[k], scalar=1.0, in1=yb[k], op0=ALU.mult, op1=ALU.mult, accum_out=sm3[:, 0:1]),
                     reads=[Byb[k], Bsm3], writes=[Bjunk, Bsm3])
                P.op(ACT, lambda e: e.activation(out=sm3[:, 1:2], in_=sm3[:, 0:1], func=AF.Sqrt, scale=1.0 / DM, bias=epsb), reads=[Bsm3], writes=[Bsm3])
                P.op(DVE, lambda e: e.reciprocal(out=sm3[:, 2:3], in_=sm3[:, 1:2]), reads=[Bsm3], writes=[Bsm3])
                P.op(DVE, lambda e, k=k: e.scalar_tensor_tensor(out=yb[k], in0=yb[k], scalar=sm3[:, 2:3], in1=nfb, op0=ALU.mult, op1=ALU.mult),
                     reads=[Byb[k], Bsm3, Bc], writes=[Byb[k]])
                P.dma(SP, y[row0:row0 + 128, :], yb[k], reads=[Byb[k]], writes=[By], key="yout%d" % k)
        P.barrier()
        P.emit()
    return nc


def _rot_cols(w):
    k, n = w.shape
    w4 = w.reshape(k, n // 64, 2, 32)
    return np.ascontiguousarray(w4[:, :, ::-1, :]).reshape(k, n)


def _prep_inputs(x, positions, norm_in, w_in, diff_lambda_q1, diff_lambda_k1, diff_lambda_q2, diff_lambda_k2,
                 diff_subln, mla_q_norm, w_uq, mla_kv_norm, w_ukv, w_proj_diff, w_proj_mla, w_out, norm_final):
    f32 = np.float32
    x = np.asarray(x, f32)[0]
    positions = np.asarray(positions)[0].astype(np.int32)
    w_in = np.asarray(w_in, f32)[0]
    o = 0
    sl = {}
    for name, n in (("dq", 512), ("dk", 512), ("dv", 512), ("dgate", 512), ("cq", 384), ("ckv", 256), ("kr", 64),
                    ("mgate", 512), ("gd", 1024), ("gm", 1024)):
        sl[name] = w_in[:, o:o + n]
        o += n
    kr2 = np.concatenate([sl["kr"], sl["kr"]], 1)
    wfm = np.concatenate([sl["dq"], sl["dk"], kr2, sl["cq"], sl["ckv"]], 1)
    assert wfm.shape[1] == NFM
    wg3 = np.concatenate([sl["dgate"], sl["mgate"], sl["gd"], sl["gm"]], 1)
    uq = np.asarray(w_uq, f32)[0].reshape(384, 4, 192)
    uq_n = uq[:, :, :128].reshape(384, 512)
    uq_r = uq[:, :, 128:].reshape(384, 256)
    wuq = np.concatenate([uq_n, uq_r], 1)
    ukv = np.asarray(w_ukv, f32)[0].reshape(256, 4, 256)
    wukv = np.concatenate([ukv[:, :, :128].reshape(256, 512), ukv[:, :, 128:].reshape(256, 512)], 1)
    cst = np.zeros((128, 16), f32)
    cst[:, 0:8] = np.asarray(norm_in, f32)[0].reshape(8, 128).T
    cst[:, 8:11] = np.asarray(mla_q_norm, f32)[0].reshape(3, 128).T
    cst[:, 11:13] = np.asarray(mla_kv_norm, f32)[0].reshape(2, 128).T
    inv_freq = (np.float32(10000.0) ** (-np.arange(0, 64, 2, dtype=np.float32) / np.float32(64))).astype(f32)
    pidx = np.arange(128)
    cst[:, 13] = inv_freq[pidx % 32]
    cst[:, 14] = np.where((pidx % 64) < 32, -1.0, 1.0)
    cst[:, 15] = np.asarray(diff_subln, f32).reshape(128)
    perm = np.zeros((128, 128), f32)
    perm[np.where((pidx % 64) < 32, pidx + 32, pidx - 32), pidx] = 1.0
    lamv = np.concatenate([np.asarray(a, f32)[0] for a in (diff_lambda_q1, diff_lambda_k1, diff_lambda_q2, diff_lambda_k2)])[None]
    shared = {
        "wfm": np.ascontiguousarray(wfm), "wdv": np.ascontiguousarray(sl["dv"]), "wuq": np.ascontiguousarray(wuq),
        "wukv": np.ascontiguousarray(wukv), "wg3": np.ascontiguousarray(wg3),
        "wpd": np.ascontiguousarray(np.asarray(w_proj_diff, f32)[0]), "wpm": np.ascontiguousarray(np.asarray(w_proj_mla, f32)[0]),
        "wout": np.ascontiguousarray(np.asarray(w_out, f32)[0]), "cst": cst, "lamv": np.ascontiguousarray(lamv),
        "subln": np.asarray(diff_subln, f32).reshape(1, 128), "nfin": np.asarray(norm_final, f32).reshape(1, DM),
        "ident": np.eye(128, dtype=f32), "perm": perm,
    }
    xt = x.reshape(TPC, NCORES, 128, DM)
    pt = positions.reshape(TPC, NCORES, 128)
    kk = np.arange(128)[:, None]
    qq = np.arange(128)[None, :]
    in_maps = []
    for c in range(NCORES):
        xc = np.ascontiguousarray(xt[:, c]).reshape(NT, DM)
        mask = np.zeros((128, 8, 128), f32)
        for j in range(8):
            r = j if USE_CC else (c + j) % NCORES
            if r > c:
                mask[:, j, :] = NEG
            elif r == c:
                mask[:, j, :] = np.where(kk > qq, NEG, 0.0)
        d = dict(shared)
        ranks = [(c + j) % NCORES for j in range(NSLOT)]
        xg = np.stack([xt[:, r].reshape(4, 512, 8, 128) for r in ranks])
        d["xT"] = np.ascontiguousarray(xg.transpose(0, 1, 4, 3, 2)).reshape(NSLOT * 4, 128, 8 * 512)
        d["xtok"] = xc
        d["pos"] = np.ascontiguousarray(np.concatenate([pt[:, r].reshape(NT) for r in ranks])).reshape(1, NSLOT * NT)
        d["mask"] = mask.reshape(128, 1024)
        in_maps.append(d)
    return in_maps


_NC_CACHE = {}


def kernel(**inputs):
    in_maps = _prep_inputs(**inputs)
    if "nc" not in _NC_CACHE:
        _NC_CACHE["nc"] = build_program(debug=bool(os.environ.get("DBG")))
    res = run_bass_kernel_spmd(_NC_CACHE["nc"], in_maps, core_ids=list(range(NCORES)))
    out = np.empty((TPC, NCORES, 128, DM), np.float32)
    for c in range(NCORES):
        out[:, c] = np.asarray(res.results[c]["y"], np.float32).reshape(TPC, 128, DM)
    return out.reshape(1, SEQ, DM)
```

```python
import contextlib
import math
import os
import numpy as np
import concourse.bass as bass
import concourse.mybir as mybir
from concourse.bass_utils import run_bass_kernel_spmd

F32 = mybir.dt.float32
BF16 = mybir.dt.bfloat16
I32 = mybir.dt.int32
AF = mybir.ActivationFunctionType
ALU = mybir.AluOpType

PE, ACT, DVE, POOL, SP = "tensor", "scalar", "vector", "gpsimd", "sync"
ENGS = (PE, ACT, DVE, POOL, SP)

NCORES = 8
SEQ = 16384
DM = 1024
NT = SEQ // NCORES
TPC = NT // 128
EPS = 1e-6
LAMBDA_INIT = 0.8 - 0.6 * math.exp(-0.3 * 0)
NEG = -30000.0

KD = 0
KN = KD + 4 * NT
KR = KN + 4 * NT
VD = KR + NT
VM = VD + 4 * TPC * 129
KVCOLS = VM + 4 * TPC * 129

NFM = 14 * 128
USE_CC = False
NSLOT = 1 if USE_CC else NCORES


class Tok:
    __slots__ = ("sem", "val")

    def __init__(self, sem, val=None):
        self.sem = sem
        self.val = val


class Buf:
    __slots__ = ("name", "w", "r")

    def __init__(self, name=""):
        self.name = name
        self.w = None
        self.r = []


class Prog:
    def __init__(self, nc, es):
        self.nc = nc
        self.es = es
        self.q = {e: [] for e in ENGS}
        self.esem = {e: es.enter_context(nc.semaphore("s_" + e)) for e in ENGS if e != SP}
        self.ecnt = {e: 0 for e in ENGS}
        self.pending = {e: [] for e in ENGS}
        self.seen = {e: {} for e in ENGS}
        self.dsem = {}
        self.dcnt = {}
        self.always = []

    def _waits(self, eng, reads, writes, exclude=None):
        toks = []
        for b in reads:
            if b.w is not None:
                toks.append(b.w)
        for b in writes:
            if b.w is not None:
                toks.append(b.w)
            toks.extend(b.r)
        need = {}
        own = self.esem.get(eng)
        for t in toks:
            if t.sem is exclude:
                continue
            if t.val is None and t.sem is own:
                continue
            assert t.val is not None, "wait on unresolved token (missing signal)"
            k = id(t.sem)
            if k not in need or need[k][1] < t.val:
                need[k] = (t.sem, t.val)
        out = []
        seen = self.seen[eng]
        for k, (sem, val) in need.items():
            if seen.get(k, 0) >= val:
                continue
            seen[k] = val
            out.append((sem, val))
        return out

    def op(self, eng, fn, reads=(), writes=(), signal=True):
        reads = list(reads) + self.always
        waits = self._waits(eng, reads, writes)
        if signal:
            self.ecnt[eng] += 1
            val = self.ecnt[eng]
            sem = self.esem[eng]
            for t in self.pending[eng]:
                t.val = val
            self.pending[eng] = []
            tok = Tok(sem, val)

            def run(e, fn=fn, waits=waits, sem=sem):
                for (s, v) in waits:
                    e.wait_ge(s, v)
                fn(e).then_inc(sem, 1)
        else:
            tok = Tok(self.esem[eng], None)
            self.pending[eng].append(tok)

            def run(e, fn=fn, waits=waits):
                for (s, v) in waits:
                    e.wait_ge(s, v)
                fn(e)
        self.q[eng].append(run)
        for b in writes:
            b.w = tok
            b.r = []
        for b in reads:
            b.r.append(tok)
        return tok

    def dma(self, queue, out, in_, reads=(), writes=(), key=None):
        if key is None:
            key = writes[0].name
        if key not in self.dsem:
            self.dsem[key] = self.es.enter_context(self.nc.semaphore("d_" + key))
            self.dcnt[key] = 0
        waits = self._waits(queue, reads, writes, exclude=self.dsem[key])
        self.dcnt[key] += 16
        sem = self.dsem[key]
        tok = Tok(sem, self.dcnt[key])

        def run(e, waits=waits, sem=sem, out=out, in_=in_):
            for (s, v) in waits:
                e.wait_ge(s, v)
            e.dma_start(out=out, in_=in_).then_inc(sem, 16)
        self.q[queue].append(run)
        for b in writes:
            b.w = tok
            b.r = []
        for b in reads:
            b.r.append(tok)
        return tok

    def raw(self, eng, fn):
        self.q[eng].append(fn)

    def all_tokens(self):
        toks = [Tok(self.esem[e], self.ecnt[e]) for e in self.esem if self.ecnt[e] > 0]
        toks += [Tok(self.dsem[k], self.dcnt[k]) for k in self.dsem]
        return toks

    def barrier(self, engines=ENGS):
        toks = self.all_tokens()
        for eng in engines:
            lst = []
            seen = self.seen[eng]
            for t in toks:
                k = id(t.sem)
                if seen.get(k, 0) >= t.val:
                    continue
                seen[k] = t.val
                lst.append((t.sem, t.val))

            def run(e, lst=lst):
                for (s, v) in lst:
                    e.wait_ge(s, v)
            self.q[eng].append(run)

    def emit(self):
        with self.nc.Block() as block:
            @block.tensor
            def _(e):
                for f in self.q[PE]:
                    f(e)

            @block.scalar
            def _(e):
                for f in self.q[ACT]:
                    f(e)

            @block.vector
            def _(e):
                for f in self.q[DVE]:
                    f(e)

            @block.gpsimd
            def _(e):
                for f in self.q[POOL]:
                    f(e)

            @block.sync
            def _(e):
                for f in self.q[SP]:
                    f(e)


class Arena:
    def __init__(self, ap, nbytes):
        self.ap = ap
        self.nbytes = nbytes
        self.cur = 0

    def at(self, off, n, dt):
        size = 2 if dt == BF16 else 4
        nb = (n * size + 31) // 32 * 32
        assert off % 4 == 0 and off + nb <= self.nbytes, ("arena overflow", off, nb, self.nbytes)
        v = self.ap[:, off // 4:(off + nb) // 4]
        if dt != F32:
            v = v.bitcast(dt)
        return v[:, 0:n], off + nb

    def alloc(self, n, dt):
        v, self.cur = self.at(self.cur, n, dt)
        return v


def build_program(debug=False, stop=0):
    nc = bass.Bass("TRN2", target_bir_lowering=False)

    def din(name, shape, dt=F32):
        return nc.dram_tensor(name, list(shape), dt, kind="ExternalInput").ap()

    xT = din("xT", [NSLOT * 4, 128, 8 * 512])
    xtok = din("xtok", [NT, DM])
    pos = din("pos", [1, NSLOT * NT], I32)
    wfm = din("wfm", [DM, NFM])
    wdv = din("wdv", [DM, 512])
    wuq = din("wuq", [384, 768])
    perm_d = din("perm", [128, 128])
    wukv = din("wukv", [256, 1024])
    wg3 = din("wg3", [DM, 3072])
    wpd = din("wpd", [512, DM])
    wpm = din("wpm", [512, DM])
    wout = din("wout", [DM, DM])
    cst = din("cst", [128, 16])
    lamv = din("lamv", [1, 256])
    subln = din("subln", [1, 128])
    nfin = din("nfin", [1, DM])
    ident_d = din("ident", [128, 128])
    mask_d = din("mask", [128, 1024])
    y = nc.dram_tensor("y", [NT, DM], F32, kind="ExternalOutput").ap()
    kv_in = nc.dram_tensor("kv_in", [128, KVCOLS], BF16, kind="Internal").ap()
    kv_all = nc.dram_tensor("kv_all", [128 * NCORES, KVCOLS], BF16, kind="Internal").ap()
    hT_d = nc.dram_tensor("hT_d", [128, 8 * NT], BF16, kind="Internal").ap()
    dbg = {}
    if debug:
        dbg["kv"] = nc.dram_tensor("dbg_kv", [128, KVCOLS], BF16, kind="ExternalOutput").ap()
        dbg["qt"] = nc.dram_tensor("dbg_qt", [128, 10 * NT], BF16, kind="ExternalOutput").ap()
        dbg["ot"] = nc.dram_tensor("dbg_ot", [128, 8 * NT], BF16, kind="ExternalOutput").ap()

    with contextlib.ExitStack() as es:
        P = Prog(nc, es)
        ARENA_BYTES = 211968
        arena_t = es.enter_context(nc.sbuf_tensor("arena", [128, ARENA_BYTES // 4], F32))
        A = Arena(arena_t[:], ARENA_BYTES)
        ps = es.enter_context(nc.psum_tensor("ps", [128, 4096], F32))
        cc_sem = es.enter_context(nc.semaphore("cc"))

        def bank(b):
            return ps[:, b * 512:(b + 1) * 512]

        psb = [Buf("psb%d" % i) for i in range(8)]
        rr = [0]

        def next_bank():
            b = rr[0] % 8
            rr[0] += 1
            return b

        ident = A.alloc(128, BF16)
        ones = A.alloc(128, BF16)
        maskb = A.alloc(1024, BF16).rearrange("p (r q) -> p r q", r=8)
        cstt = A.alloc(16, F32)
        lamt = A.alloc(256, F32)
        lamw = A.alloc(8, F32)
        sublnb = A.alloc(128, F32)
        nfb = A.alloc(DM, F32)
        half = A.alloc(8, F32)
        junk64 = A.alloc(64, F32)
        Bc = Buf("consts")

        def pbcast(ap):
            v = ap.partition_broadcast(128)
            if len(v.shape) == 3:
                v = v.rearrange("p o n -> p (o n)")
            return v

        P.dma(POOL, ident, ident_d, writes=[Bc], key="c0")
        P.dma(POOL, maskb.rearrange("p r q -> p (r q)"), mask_d, writes=[Bc], key="c0")
        P.dma(POOL, cstt, cst, writes=[Bc], key="c0")
        P.dma(POOL, lamt, pbcast(lamv), writes=[Bc], key="c0")
        P.dma(POOL, sublnb, pbcast(subln), writes=[Bc], key="c0")
        P.dma(POOL, nfb, pbcast(nfin), writes=[Bc], key="c0")
        P.op(DVE, lambda e: e.memset(ones, 1.0), writes=[Bc])
        P.op(DVE, lambda e: e.memset(half[:, 0:1], EPS), writes=[Bc])
        P.op(DVE, lambda e: e.memset(half[:, 1:2], 0.0), writes=[Bc])
        P.op(DVE, lambda e: e.memset(half[:, 2:3], float(np.pi / 2)), writes=[Bc])
        P.op(DVE, lambda e: e.memset(half[:, 3:4], -0.5), writes=[Bc])
        epsb = half[:, 0:1]
        pio2 = half[:, 2:3]
        mhalf = half[:, 3:4]
        P.op(DVE, lambda e: e.scalar_tensor_tensor(out=junk64, in0=lamt[:, 0:64], scalar=1.0, in1=lamt[:, 64:128],
                                                   op0=ALU.mult, op1=ALU.mult, accum_out=lamw[:, 0:1]),
             reads=[Bc], writes=[Bc])
        P.op(DVE, lambda e: e.scalar_tensor_tensor(out=junk64, in0=lamt[:, 128:192], scalar=1.0, in1=lamt[:, 192:256],
                                                   op0=ALU.mult, op1=ALU.mult, accum_out=lamw[:, 1:2]),
             reads=[Bc], writes=[Bc])
        P.op(ACT, lambda e: e.activation(out=lamw[:, 2:4], in_=lamw[:, 0:2], func=AF.Exp), reads=[Bc], writes=[Bc])
        P.op(DVE, lambda e: e.scalar_tensor_tensor(out=lamw[:, 4:5], in0=lamw[:, 3:4], scalar=-LAMBDA_INIT, in1=lamw[:, 2:3],
                                                   op0=ALU.add, op1=ALU.subtract), reads=[Bc], writes=[Bc])
        neglam = lamw[:, 4:5]
        Blam = Bc
        P.op(DVE, lambda e: e.tensor_scalar(out=sublnb, in0=sublnb, scalar1=1.0 - LAMBDA_INIT, scalar2=None, op0=ALU.mult),
             reads=[Bc], writes=[Bc])
        P.op(DVE, lambda e: e.tensor_scalar(out=cstt[:, 15:16], in0=cstt[:, 15:16], scalar1=1.0 - LAMBDA_INIT, scalar2=None, op0=ALU.mult),
             reads=[Bc], writes=[Bc])
        P.always = [Bc]
        g_in = cstt[:, 0:8]
        g_q = cstt[:, 8:11]
        g_kv = cstt[:, 11:13]
        invf = cstt[:, 13:14]
        sgn = cstt[:, 14:15]
        sublnc = cstt[:, 15:16]
        CONST_END = A.cur

        QTd = A.alloc(4 * NT, BF16).rearrange("p (h t) -> p h t", h=4)
        QTn = A.alloc(4 * NT, BF16).rearrange("p (h t) -> p h t", h=4)
        QTr = A.alloc(2 * NT, BF16).rearrange("p (h t) -> p h t", h=2)
        BQ = Buf("QT")
        X0 = A.cur

        Wfm = A.alloc(8 * NFM, BF16).rearrange("p (c n) -> p c n", c=8)
        Wdv = A.alloc(8 * 512, BF16).rearrange("p (c n) -> p c n", c=8)
        Wuq = A.alloc(3 * 768, BF16).rearrange("p (c n) -> p c n", c=3)
        Wukv = A.alloc(2 * 1024, BF16).rearrange("p (c n) -> p c n", c=2)
        permb = A.alloc(128, BF16)
        xTg = A.alloc(8 * 512, F32).rearrange("p (c n) -> p c n", c=8)
        sqg = A.alloc(8 * 512, BF16).rearrange("p (c n) -> p c n", c=8)
        sq2 = A.alloc(5 * 512, BF16).rearrange("p (c n) -> p c n", c=5)
        rstd2 = [A.alloc(512, F32) for _ in range(2)]
        hTg2 = [A.alloc(8 * 512, BF16).rearrange("p (c n) -> p c n", c=8) for _ in range(2)]
        cos2 = [A.alloc(512, F32) for _ in range(2)]
        sin2 = [A.alloc(512, F32) for _ in range(2)]
        tA = A.alloc(512, F32)
        tB = A.alloc(512, F32)
        posi = A.alloc(512, I32)
        mA2 = [A.alloc(512, F32) for _ in range(2)]
        mB2 = [A.alloc(512, F32) for _ in range(2)]
        rawbf = [A.alloc(512, BF16) for _ in range(2)]
        cqraw = A.alloc(3 * 512, F32).rearrange("p (c n) -> p c n", c=3)
        cqn = A.alloc(3 * 512, BF16).rearrange("p (c n) -> p c n", c=3)
        rstdq = A.alloc(512, F32)
        ckvraw = A.alloc(2 * 512, F32).rearrange("p (c n) -> p c n", c=2)
        ckvn = A.alloc(2 * 512, BF16).rearrange("p (c n) -> p c n", c=2)
        rstdkv = A.alloc(512, F32)
        kTd = A.alloc(4 * 512, BF16).rearrange("p (h n) -> p h n", h=4)
        kTn = A.alloc(4 * 512, BF16).rearrange("p (h n) -> p h n", h=4)
        kTr = A.alloc(512, BF16)
        Vd = A.alloc(4 * 4 * 129, BF16).rearrange("p (t h j) -> p t h j", t=4, h=4)
        Vm = A.alloc(4 * 4 * 129, BF16).rearrange("p (t h j) -> p t h j", t=4, h=4)
        P1_END = A.cur

        BW = Buf("w1")
        wfm_v = wfm.rearrange("(c p) n -> p c n", p=128)
        P.dma(POOL, permb, perm_d, writes=[BW], key="w1")
        for c in range(8):
            P.dma(POOL, Wfm[:, c, :], wfm_v[:, c, :], writes=[BW], key="w1")
        P.dma(POOL, Wdv, wdv.rearrange("(c p) n -> p c n", p=128), writes=[BW], key="w1")
        P.dma(POOL, Wuq, wuq.rearrange("(c p) n -> p c n", p=128), writes=[BW], key="w1")
        P.dma(POOL, Wukv, wukv.rearrange("(c p) n -> p c n", p=128), writes=[BW], key="w1")

        BxT, Bsq, Bsq2, BtA, BtB, Bpos = [Buf(n) for n in "xT sq sq2 tA tB posi".split()]
        Brstd2 = [Buf("rstd0"), Buf("rstd1")]
        BhT2 = [Buf("hT0"), Buf("hT1")]
        Btab2 = [Buf("tab0"), Buf("tab1")]
        BmA2 = [Buf("mA0"), Buf("mA1")]
        BmB2 = [Buf("mB0"), Buf("mB1")]
        Braw = [Buf("raw0"), Buf("raw1")]
        Bcq, Bcqn, Brq, Bckv, Bckvn, Brkv = [Buf(n) for n in "cq cqn rq ckv ckvn rkv".split()]
        BkTd, BkTn, BkTr, BVd, BVm = [Buf(n) for n in "kTd kTn kTr Vd Vm".split()]
        Bkvin = Buf("kvin")
        BhTd = Buf("hTd")
        for tt in range(4):
            P.op(POOL, lambda e, tt=tt: e.memset(Vd[:, tt, :, 128:129], 1.0), writes=[BVd])
            P.op(POOL, lambda e, tt=tt: e.memset(Vm[:, tt, :, 128:129], 1.0), writes=[BVm])

        kvin_k = lambda dst, base: dst[:, base:base + 4 * NT].rearrange("p (h t) -> p h t", h=4)
        kvin_v = lambda dst, base: dst[:, base:base + 4 * TPC * 129].rearrange("p (h m j) -> p m h j", h=4, m=TPC)

        def rstd_from_psum(b, nfeat, rdst, Brd):
            if "s" in os.environ.get("XP", ""):
                P.op(ACT, lambda e: e.activation(out=rdst, in_=bank(b), func=AF.Sqrt, scale=1.0 / nfeat, bias=epsb),
                     reads=[psb[b]], writes=[Brd])
                P.op(DVE, lambda e: e.reciprocal(out=rdst, in_=rdst), reads=[Brd], writes=[Brd])
                return
            P.op(ACT, lambda e: e.activation(out=rdst, in_=bank(b), func=AF.Ln, scale=1.0 / nfeat, bias=epsb),
                 reads=[psb[b]], writes=[Brd])
            P.op(ACT, lambda e: e.activation(out=rdst, in_=rdst, func=AF.Exp, scale=-0.5), reads=[Brd], writes=[Brd])

        def proj_fm(W, ncs, col0, rhs, Brhs, ncols=128, BWx=None):
            BWx = BWx or BW
            b = next_bank()
            for c in range(ncs):
                P.op(PE, lambda e, c=c: e.matmul(bank(b)[0:ncols, :], lhsT=W[:, c, col0:col0 + ncols], rhs=rhs[:, c, :],
                                                 start=(c == 0), stop=(c == ncs - 1)),
                     reads=[BWx, Brhs], writes=[psb[b]] if c in (0, ncs - 1) else [], signal=(c == ncs - 1))
            return b

        rawctr = [0]

        def rope_a(W, ncs, col0, rhs, Brhs):
            bA = proj_fm(W, ncs, col0, rhs, Brhs)
            k = rawctr[0] % 2
            rawctr[0] += 1
            P.op(ACT, lambda e: e.activation(out=rawbf[k], in_=bank(bA), func=AF.Copy), reads=[psb[bA]], writes=[Braw[k]])
            return (bA, k)

        def rope_b(st, dst, Bdst, nb):
            bA, k = st
            bB = next_bank()
            P.op(PE, lambda e: e.matmul(bank(bB), lhsT=permb, rhs=rawbf[k], start=True, stop=True), reads=[Braw[k], BW], writes=[psb[bB]])
            mA, mB, BmA, BmB = mA2[k], mB2[k], BmA2[k], BmB2[k]
            P.op(DVE, lambda e: e.tensor_tensor(out=mA, in0=bank(bA), in1=cos2[nb], op=ALU.mult), reads=[psb[bA], Btab2[nb], Braw[k]], writes=[BmA])
            P.op(DVE, lambda e: e.tensor_tensor(out=mB, in0=bank(bB), in1=sin2[nb], op=ALU.mult), reads=[psb[bB], Btab2[nb]], writes=[BmB])
            P.op(POOL, lambda e: e.tensor_tensor(out=dst, in0=mA, in1=mB, op=ALU.add), reads=[BmA, BmB], writes=[Bdst])

        def rope_tiles(specs, nb):
            prev = None
            for sp_ in specs:
                st = rope_a(*sp_[0:5])
                if prev is not None:
                    rope_b(prev[0], prev[1], prev[2], nb)
                prev = (st, sp_[5], sp_[6])
            rope_b(prev[0], prev[1], prev[2], nb)

        groups = [(slot, G) for slot in range(NSLOT) for G in range(4)]
        if os.environ.get("NG"):
            groups = groups[:int(os.environ["NG"])]

        def load_x(n):
            slot, G = groups[n]
            P.dma(SP, xTg.rearrange("p c t -> p (c t)"), xT[slot * 4 + G], writes=[BxT])

        def norm(n, part=0):
            slot, G = groups[n]
            nb = n % 2
            full = slot == 0
            hTg = hTg2[nb]
            tg = slice(G * 512, (G + 1) * 512)
            if part in (0, 1):
                P.op(ACT, lambda e: e.activation(out=sqg, in_=xTg, func=AF.Square), reads=[BxT], writes=[Bsq])
                if part == 1:
                    return
            if part == 3:
                return norm_tables(n)
            b = next_bank()
            for c in range(8):
                P.op(PE, lambda e, c=c, b=b: e.matmul(bank(b), lhsT=ones, rhs=sqg[:, c, :], start=(c == 0), stop=(c == 7)),
                     reads=[Bsq], writes=[psb[b]] if c in (0, 7) else [], signal=(c == 7))
            rstd_from_psum(b, 1024.0, rstd2[nb], Brstd2[nb])
            for c in range(8):
                P.op(DVE, lambda e, c=c: e.scalar_tensor_tensor(out=hTg[:, c, :], in0=xTg[:, c, :], scalar=g_in[:, c:c + 1],
                                                                 in1=rstd2[nb], op0=ALU.mult, op1=ALU.mult),
                     reads=[BxT, Brstd2[nb]], writes=[BhT2[nb]])
            if n + 1 < len(groups):
                load_x(n + 1)
            if full:
                P.dma(SP, hT_d.rearrange("p (c t) -> p c t", c=8)[:, :, tg], hTg, reads=[BhT2[nb]], writes=[BhTd], key="hTd%d" % nb)
            if part == 2:
                return
            norm_tables(n)

        def load_pos(n):
            slot, G = groups[n]
            P.dma(SP, posi, pbcast(pos[:, slot * NT + G * 512:slot * NT + (G + 1) * 512]), writes=[Bpos], key="tab")

        def norm_tables(n):
            slot, G = groups[n]
            nb = n % 2
            P.op(DVE, lambda e: e.tensor_copy(out=tA, in_=posi), reads=[Bpos], writes=[BtA])
            P.op(DVE, lambda e: e.tensor_scalar(out=tA, in0=tA, scalar1=invf, scalar2=None, op0=ALU.mult), reads=[BtA], writes=[BtA])
            ki = posi
            P.op(DVE, lambda e: e.tensor_scalar(out=ki, in0=tA, scalar1=float(1.0 / (2 * np.pi)), scalar2=None, op0=ALU.mult),
                 reads=[BtA], writes=[Bpos])
            P.op(DVE, lambda e: e.tensor_copy(out=tB, in_=ki), reads=[Bpos], writes=[BtB])
            C1 = 6.28125
            C2 = float(np.float32(2 * np.pi - 6.28125))
            P.op(DVE, lambda e: e.scalar_tensor_tensor(out=tA, in0=tB, scalar=-C1, in1=tA, op0=ALU.mult, op1=ALU.add), reads=[BtB, BtA], writes=[BtA])
            P.op(DVE, lambda e: e.scalar_tensor_tensor(out=tA, in0=tB, scalar=-C2, in1=tA, op0=ALU.mult, op1=ALU.add), reads=[BtB, BtA], writes=[BtA])
            PI_LO = 3.1415925
            P.op(DVE, lambda e: e.tensor_scalar(out=tA, in0=tA, scalar1=PI_LO, scalar2=-PI_LO, op0=ALU.min, op1=ALU.max), reads=[BtA], writes=[BtA])
            P.op(ACT, lambda e: e.activation(out=sin2[nb], in_=tA, func=AF.Sin, scale=sgn), reads=[BtA], writes=[Btab2[nb]])
            P.op(DVE, lambda e: e.tensor_scalar(out=tB, in0=tA, scalar1=float(np.pi / 2), scalar2=float(np.pi), op0=ALU.add, op1=ALU.is_gt),
                 reads=[BtA], writes=[BtB])
            P.op(DVE, lambda e: e.scalar_tensor_tensor(out=tB, in0=tB, scalar=-float(2 * np.pi), in1=tA, op0=ALU.mult, op1=ALU.add),
                 reads=[BtB, BtA], writes=[BtB])
            P.op(DVE, lambda e: e.tensor_scalar(out=tB, in0=tB, scalar1=float(np.pi / 2), scalar2=PI_LO, op0=ALU.add, op1=ALU.min), reads=[BtB], writes=[BtB])
            P.op(DVE, lambda e: e.tensor_scalar(out=tB, in0=tB, scalar1=-PI_LO, scalar2=None, op0=ALU.max), reads=[BtB], writes=[BtB])
            P.op(ACT, lambda e: e.activation(out=cos2[nb], in_=tB, func=AF.Sin), reads=[BtB], writes=[Btab2[nb]])

        def proj(n):
            slot, G = groups[n]
            nb = n % 2
            full = slot == 0
            hTg, BhT = hTg2[nb], BhT2[nb]
            tg = slice(G * 512, (G + 1) * 512)
            kvdst = kv_in if USE_CC else kv_all[slot * 128:(slot + 1) * 128, :]
            more = n + 1 < len(groups)
            if more:
                load_pos(n + 1)

            def vdiff(tts):
                for tt in tts:
                    b = next_bank()
                    for c in range(8):
                        P.op(PE, lambda e, c=c, b=b, tt=tt: e.matmul(bank(b), lhsT=hTg[:, c, tt * 128:(tt + 1) * 128], rhs=Wdv[:, c, :],
                                                                     start=(c == 0), stop=(c == 7)),
                             reads=[BW, BhT], writes=[psb[b]] if c in (0, 7) else [], signal=(c == 7))
                    P.op(ACT, lambda e, b=b, tt=tt: e.activation(out=Vd[:, tt, :, 0:128], in_=bank(b).rearrange("p (h j) -> p h j", h=4), func=AF.Copy),
                         reads=[psb[b]], writes=[BVd])

            for j in range(2):
                b = proj_fm(Wfm, 8, (12 + j) * 128, hTg, BhT)
                P.op(ACT, lambda e, j=j, b=b: e.activation(out=ckvraw[:, j, :], in_=bank(b), func=AF.Copy), reads=[psb[b]], writes=[Bckv])
                P.op(ACT, lambda e, j=j, b=b: e.activation(out=sq2[:, 3 + j, :], in_=bank(b), func=AF.Square), reads=[psb[b]], writes=[Bsq2])
            if full:
                for j in range(3):
                    b = proj_fm(Wfm, 8, (9 + j) * 128, hTg, BhT)
                    P.op(ACT, lambda e, j=j, b=b: e.activation(out=cqraw[:, j, :], in_=bank(b), func=AF.Copy), reads=[psb[b]], writes=[Bcq])
                    P.op(ACT, lambda e, j=j, b=b: e.activation(out=sq2[:, j, :], in_=bank(b), func=AF.Square), reads=[psb[b]], writes=[Bsq2])
            specs = [(Wfm, 8, (4 + h) * 128, hTg, BhT, kTd[:, h, :], BkTd) for h in range(4)]
            specs.append((Wfm, 8, 8 * 128, hTg, BhT, kTr, BkTr))
            rope_tiles(specs[0:2], nb)
            if more:
                norm(n + 1, part=1)
            b = next_bank()
            for j in range(2):
                P.op(PE, lambda e, j=j, b=b: e.matmul(bank(b), lhsT=ones, rhs=sq2[:, 3 + j, :], start=(j == 0), stop=(j == 1)),
                     reads=[Bsq2], writes=[psb[b]] if j in (0, 1) else [], signal=(j == 1))
            rstd_from_psum(b, 256.0, rstdkv, Brkv)
            for j in range(2):
                P.op(DVE, lambda e, j=j: e.scalar_tensor_tensor(out=ckvn[:, j, :], in0=ckvraw[:, j, :], scalar=g_kv[:, j:j + 1], in1=rstdkv,
                                                                op0=ALU.mult, op1=ALU.mult), reads=[Bckv, Brkv], writes=[Bckvn])
            if full:
                b = next_bank()
                for j in range(3):
                    P.op(PE, lambda e, j=j, b=b: e.matmul(bank(b), lhsT=ones, rhs=sq2[:, j, :], start=(j == 0), stop=(j == 2)),
                         reads=[Bsq2], writes=[psb[b]] if j in (0, 2) else [], signal=(j == 2))
                rstd_from_psum(b, 384.0, rstdq, Brq)
                for j in range(3):
                    P.op(DVE, lambda e, j=j: e.scalar_tensor_tensor(out=cqn[:, j, :], in0=cqraw[:, j, :], scalar=g_q[:, j:j + 1], in1=rstdq,
                                                                    op0=ALU.mult, op1=ALU.mult), reads=[Bcq, Brq], writes=[Bcqn])
            rope_tiles(specs[2:5], nb)
            P.dma(SP, kvin_k(kvdst, KD)[:, :, tg], kTd, reads=[BkTd], writes=[Bkvin], key="kvin_kTd")
            P.dma(SP, kvdst[:, KR + G * 512:KR + (G + 1) * 512], kTr, reads=[BkTr], writes=[Bkvin], key="kvin_kTr")
            if more:
                norm(n + 1, part=2)
            vdiff([0, 1])
            if full:
                rope_tiles([(Wfm, 8, h * 128, hTg, BhT, QTd[:, h, tg], BQ) for h in range(4)], nb)
            vdiff([2, 3])
            for tt in range(4):
                P.dma(SP, kvin_v(kvdst, VD)[:, 4 * G + tt, :, :], Vd[:, tt, :, :], reads=[BVd], writes=[Bkvin], key="kvin_Vd")
            if full:
                for h in range(4):
                    b = proj_fm(Wuq, 3, h * 128, cqn, Bcqn)
                    P.op(ACT, lambda e, h=h, b=b, tg=tg: e.activation(out=QTn[:, h, tg], in_=bank(b), func=AF.Copy), reads=[psb[b]], writes=[BQ])
                rope_tiles([(Wuq, 3, (4 + pr) * 128, cqn, Bcqn, QTr[:, pr, tg], BQ) for pr in range(2)], nb)
            for h in range(4):
                b = proj_fm(Wukv, 2, h * 128, ckvn, Bckvn)
                P.op(ACT, lambda e, h=h, b=b: e.activation(out=kTn[:, h, :], in_=bank(b), func=AF.Copy), reads=[psb[b]], writes=[BkTn])
            P.dma(SP, kvin_k(kvdst, KN)[:, :, tg], kTn, reads=[BkTn], writes=[Bkvin], key="kvin_kTn")
            for tt in range(4):
                b = next_bank()
                for j in range(2):
                    P.op(PE, lambda e, j=j, b=b, tt=tt: e.matmul(bank(b), lhsT=ckvn[:, j, tt * 128:(tt + 1) * 128], rhs=Wukv[:, j, 512:1024],
                                                                 start=(j == 0), stop=(j == 1)),
                         reads=[BW, Bckvn], writes=[psb[b]] if j in (0, 1) else [], signal=(j == 1))
                P.op(DVE, lambda e, b=b, tt=tt: e.tensor_copy(out=Vm[:, tt, :, 0:128], in_=bank(b).rearrange("p (h j) -> p h j", h=4)),
                     reads=[psb[b]], writes=[BVm])
            for tt in range(4):
                P.dma(SP, kvin_v(kvdst, VM)[:, 4 * G + tt, :, :], Vm[:, tt, :, :], reads=[BVm], writes=[Bkvin], key="kvin_Vm")
            if more:
                norm(n + 1, part=3)

        load_x(0)
        load_pos(0)
        norm(0)
        for n in range(len(groups)):
            proj(n)

        P.barrier()
        if stop == 1:
            P.dma(SP, dbg["kv"], kv_in if USE_CC else kv_all[0:128, :], writes=[Buf("dbgkv")], key="dbg")
            P.dma(SP, dbg["qt"][:, 0:4 * NT], QTd.rearrange("p h t -> p (h t)"), reads=[BQ], writes=[Buf("dbgq")], key="dbg")
            P.dma(SP, dbg["qt"][:, 4 * NT:8 * NT], QTn.rearrange("p h t -> p (h t)"), reads=[BQ], writes=[Buf("dbgq")], key="dbg")
            P.dma(SP, dbg["qt"][:, 8 * NT:10 * NT], QTr.rearrange("p h t -> p (h t)"), reads=[BQ], writes=[Buf("dbgq")], key="dbg")
            P.barrier()
            P.emit()
            return nc
        if USE_CC:
            P.raw(POOL, lambda e: e.collective_compute("AllGather", ALU.bypass, replica_groups=[list(range(NCORES))],
                                                       ins=[kv_in], outs=[kv_all]).then_inc(cc_sem, 1))
            for eng in ENGS:
                P.raw(eng, lambda e: e.wait_ge(cc_sem, 1))
        XP = os.environ.get("XP", "")
        if "a" in XP:
            P.barrier()
        if "b" in XP:
            scr = nc.dram_tensor("scr", [128, KVCOLS], BF16, kind="Internal").ap()
            P.dma(SP, scr, kv_in, writes=[Buf("scr")], key="scr")
            P.barrier()
        if debug and "B" in os.environ.get("DBG", "BC"):
            P.dma(SP, dbg["kv"], kv_in if USE_CC else kv_all[0:128, :], writes=[Buf("dbgkv")], key="dbg")
            P.dma(SP, dbg["qt"][:, 0:4 * NT], QTd.rearrange("p h t -> p (h t)"), reads=[BQ], writes=[Buf("dbgq")], key="dbg")
            P.dma(SP, dbg["qt"][:, 4 * NT:8 * NT], QTn.rearrange("p h t -> p (h t)"), reads=[BQ], writes=[Buf("dbgq")], key="dbg")
            P.dma(SP, dbg["qt"][:, 8 * NT:10 * NT], QTr.rearrange("p h t -> p (h t)"), reads=[BQ], writes=[Buf("dbgq")], key="dbg")
            P.barrier()

        A.cur = X0
        oT = A.alloc(8 * NT, BF16).rearrange("p (h t) -> p h t", h=8)
        BoT = Buf("oT")
        X1 = A.cur
        NKB = 3
        KTc = [A.alloc(8 * 512, BF16).rearrange("p (r n) -> p r n", r=8) for _ in range(NKB)]
        KRc = [A.alloc(8 * 512, BF16).rearrange("p (r n) -> p r n", r=8) for _ in range(NKB)]
        Vc = [A.alloc(8 * 4 * 129, BF16).rearrange("p (r n) -> p r n", r=8) for _ in range(NKB)]
        BKV = [Buf("kvc%d" % i) for i in range(NKB)]
        for i in range(NKB):
            P.op(POOL, lambda e, i=i: e.memset(KTc[i][64:128, :, :], 0.0), writes=[BKV[i]])
            P.op(POOL, lambda e, i=i: e.memset(KRc[i][0:64, :, :], 0.0), writes=[BKV[i]])
        NPT = 4
        PT = [A.alloc(1024, BF16) for _ in range(NPT)]
        BPT = [Buf("pt%d" % i) for i in range(NPT)]
        NPS = 2
        PSm = [A.alloc(1024, BF16) for _ in range(NPS)]
        BPSm = [Buf("psm%d" % i) for i in range(NPS)]
        Osb = A.alloc(4 * 512, F32)
        BOsb = Buf("Osb")
        ofin = A.alloc(512, F32)
        osq = A.alloc(512, BF16)
        rsb = A.alloc(512, F32)
        cm1 = A.alloc(1024, F32)
        cmh = A.alloc(512, F32)
        Bofin, Bosq, Brsb = Buf("ofin"), Buf("osq"), Buf("rsb")
        P.op(POOL, lambda e: e.memset(cm1, -1.0), writes=[Bc])
        P.op(POOL, lambda e: e.memset(cmh, -0.5), writes=[Bc])
        BO = Buf("Oacc")
        BS = [Buf("S0"), Buf("S1")]
        kvall_v = kv_all.rearrange("(r p) n -> p r n", p=128)
        NS = 8

        def Sslot(s):
            return ps[:, 2048 + s * 1024:2048 + (s + 1) * 1024]

        jobs = []
        for h in range(4):
            for u in range(4):
                for ci in range(u + 1):
                    jobs.append(("d", h, u, ci))
        for h in range(4):
            for p_ in range(2):
                for ci in range(2 * p_ + 2):
                    jobs.append(("m", h, p_, ci))

        def load_chunk(ji):
            kind, h, _, ci = jobs[ji]
            s = ji % NKB
            g0 = 4 * ci
            if kind == "d":
                P.dma(SP, KTc[s][0:64, :, :], kvall_v[0:64, :, KD + h * NT + g0 * 128:KD + h * NT + g0 * 128 + 512], writes=[BKV[s]])
                P.dma(SP, KRc[s][64:128, :, :], kvall_v[64:128, :, KD + h * NT + g0 * 128:KD + h * NT + g0 * 128 + 512], writes=[BKV[s]])
                P.dma(SP, Vc[s], kvall_v[:, :, VD + (h * TPC + g0) * 129:VD + (h * TPC + g0 + 4) * 129], writes=[BKV[s]])
            else:
                P.dma(SP, KTc[s], kvall_v[:, :, KN + h * NT + g0 * 128:KN + h * NT + g0 * 128 + 512], writes=[BKV[s]])
                lo = (h % 2) * 64
                zl = 64 - lo
                P.op(POOL, lambda e, s=s, zl=zl: e.memset(KRc[s][zl:zl + 64, :, :], 0.0), writes=[BKV[s]])
                P.dma(SP, KRc[s][lo:lo + 64, :, :], kvall_v[lo:lo + 64, :, KR + g0 * 128:KR + g0 * 128 + 512], writes=[BKV[s]])
                P.dma(SP, Vc[s], kvall_v[:, :, VM + (h * TPC + g0) * 129:VM + (h * TPC + g0 + 4) * 129], writes=[BKV[s]])

        steps = []
        for ji, (kind, h, qp, ci) in enumerate(jobs):
            nq = 4 if kind == "d" else 8
            mbase = nq * qp
            for gg in range(4):
                g = 4 * ci + gg
                if g > mbase + nq - 1:
                    continue
                for r in range(8):
                    steps.append((ji, kind, h, qp, g, gg, r))
        last_step_of_pass = {}
        for t, st in enumerate(steps):
            last_step_of_pass[(st[1], st[2], st[3])] = t
        nsteps = len(steps)
        slot_of = {}
        sctr = [0]

        def take_slot():
            s = sctr[0] % 2
            sctr[0] += 1
            return s

        SCALE_D = 64 ** -0.5
        SCALE_M = 192 ** -0.5

        def geom(t):
            ji, kind, h, qp, g, gg, r = steps[t]
            nq = 4 if kind == "d" else 8
            mbase = nq * qp
            m_lo = max(mbase, g)
            cols = (mbase + nq - m_lo) * 128
            c_lo = (m_lo - mbase) * 128
            return nq, mbase, m_lo, cols, c_lo

        def issue_qk(t):
            ji, kind, h, qp, g, gg, r = steps[t]
            nq, mbase, m_lo, cols, c_lo = geom(t)
            s = ji % NKB
            sl = take_slot()
            slot_of[t] = sl
            diag = g >= mbase
            q0 = m_lo * 128
            Sv = Sslot(sl)
            kcols = slice(gg * 128, (gg + 1) * 128)
            segs = []
            c0 = 0
            if kind == "d":
                segs.append((0, cols, diag))
            else:
                while c0 < cols:
                    c1 = min(cols, (c0 // 512 + 1) * 512)
                    segs.append((c0, c1, diag and c0 == 0))
                    c0 = c1
            mm = []
            if kind == "d":
                for mp in range(2):
                    kbuf = KTc[s] if mp == 0 else KRc[s]
                    for (a, b_, dg) in segs:
                        mm.append((Sv[:, mp * 512 + a:mp * 512 + b_], kbuf[:, r, kcols], QTd[:, h, q0 + a:q0 + b_], True, not dg))
                        if dg:
                            mm.append((Sv[:, mp * 512:mp * 512 + 128], ident, maskb[:, r, :], False, True))
            else:
                rows = slice((h % 2) * 64, (h % 2) * 64 + 64)
                for (a, b_, dg) in segs:
                    mm.append((Sv[:, a:b_], KTc[s][:, r, kcols], QTn[:, h, q0 + a:q0 + b_], True, False))
                    mm.append((Sv[:, a:b_], KRc[s][:, r, kcols], QTr[:, h // 2, q0 + a:q0 + b_], False, not dg))
                    if dg:
                        mm.append((Sv[:, 0:128], ident, maskb[:, r, :], False, True))
            for i, (o_, l_, r_, st_, sp_) in enumerate(mm):
                last = i == len(mm) - 1
                P.op(PE, lambda e, o_=o_, l_=l_, r_=r_, st_=st_, sp_=sp_: e.matmul(o_, lhsT=l_, rhs=r_, start=st_, stop=sp_, skip_group_check=True),
                     reads=[BKV[s], BQ], writes=[BS[sl]] if (i == 0 or last) else [], signal=last)

        def pt_view(buf, kind, cols):
            if kind == "d":
                if cols == 512:
                    return buf[:, 0:1024]
                return buf.rearrange("p (a n) -> p a n", a=2)[:, :, 0:cols]
            return buf[:, 0:cols]

        def issue_exp(t):
            ji, kind, h, qp, g, gg, r = steps[t]
            nq, mbase, m_lo, cols, c_lo = geom(t)
            sl = slot_of[t]
            pt = t % NPT
            src = pt_view(Sslot(sl), kind, cols)
            dst = pt_view(PT[pt], kind, cols)
            sc = SCALE_D if kind == "d" else SCALE_M
            P.op(ACT, lambda e: e.activation(out=dst, in_=src, func=AF.Exp, scale=sc), reads=[BS[sl]], writes=[BPT[pt]])

        def out_segs(kind, c_lo, cols):
            if kind == "d":
                return [(mp, c_lo, c_lo + cols, mp * 512) for mp in range(2)]
            res = []
            a = c_lo
            while a < c_lo + cols:
                b_ = min(c_lo + cols, (a // 512 + 1) * 512)
                res.append((a // 512, a % 512, a % 512 + (b_ - a), a - c_lo))
                a = b_
            return res

        def issue_pv(t):
            ji, kind, h, qp, g, gg, r = steps[t]
            nq, mbase, m_lo, cols, c_lo = geom(t)
            s = ji % NKB
            pt = t % NPT
            first = (g == 0 and r == 0)
            vt = Vc[s][:, r, gg * 129:gg * 129 + 128]
            pieces = out_segs(kind, c_lo, cols)
            for i, (bk, a, b_, po) in enumerate(pieces):
                o_ = ps[:, bk * 512 + a:bk * 512 + b_]
                r_ = PT[pt][:, po:po + (b_ - a)]
                P.op(PE, lambda e, o_=o_, r_=r_, first=first: e.matmul(o_, lhsT=vt, rhs=r_, start=first, stop=False, skip_group_check=True),
                     reads=[BPT[pt], BKV[s]], writes=[BO] if i == 0 else [], signal=False)

        def issue_preadd(t):
            ji, kind, h, qp, g, gg, r = steps[t]
            nq, mbase, m_lo, cols, c_lo = geom(t)
            k = r % NS
            if k == 0:
                return
            pi = (t // NS) % NPS
            dst = pt_view(PSm[pi], kind, cols)
            a_ = pt_view(PT[(t - 1) % NPT], kind, cols) if k == 1 else dst
            b_ = pt_view(PT[t % NPT], kind, cols)
            rd = [BPT[t % NPT]] + ([BPT[(t - 1) % NPT]] if k == 1 else [BPSm[pi]])
            P.op(DVE, lambda e: e.tensor_tensor(out=dst, in0=a_, in1=b_, op=ALU.add), reads=rd, writes=[BPSm[pi]])

        def issue_den(t):
            ji, kind, h, qp, g, gg, r = steps[t]
            if r % NS != NS - 1:
                return
            nq, mbase, m_lo, cols, c_lo = geom(t)
            pi = (t // NS) % NPS
            first = (g == 0 and r == NS - 1)
            is_last = last_step_of_pass[(kind, h, qp)] == t
            pieces = out_segs(kind, c_lo, cols)
            for i, (bk, a, b_, po) in enumerate(pieces):
                o_ = ps[:, (2 + bk) * 512 + a:(2 + bk) * 512 + b_]
                r_ = PSm[pi][:, po:po + (b_ - a)]
                last = is_last and i == len(pieces) - 1
                P.op(PE, lambda e, o_=o_, r_=r_, first=first: e.matmul(o_, lhsT=ones, rhs=r_, start=first, stop=False, skip_group_check=True),
                     reads=[BPSm[pi]], writes=[BO] if (i == 0 or last) else [], signal=last)

        deferred = []

        def finish_pass(kind, h, qp, t):
            nq = 4 if kind == "d" else 8
            q0 = nq * qp * 128
            P.op(DVE, lambda e: e.tensor_copy(out=Osb[:, 0:1024], in_=ps[:, 0:1024]), reads=[BO], writes=[BOsb])
            P.op(ACT, lambda e: e.activation(out=Osb[:, 1024:2048], in_=ps[:, 1024:2048], func=AF.Ln), reads=[BO], writes=[BOsb])
            P.op(ACT, lambda e: e.activation(out=Osb[:, 1024:2048], in_=Osb[:, 1024:2048], func=AF.Exp, scale=-1.0), reads=[BOsb], writes=[BOsb])
            if kind == "m":
                P.op(POOL, lambda e: e.tensor_tensor(out=oT[:, 4 + h, q0:q0 + 1024], in0=Osb[:, 0:1024], in1=Osb[:, 1024:2048], op=ALU.mult),
                     reads=[BOsb], writes=[BoT])
                return
            P.op(POOL, lambda e: e.tensor_tensor(out=Osb[:, 0:1024], in0=Osb[:, 0:1024], in1=Osb[:, 1024:2048], op=ALU.mult), reads=[BOsb], writes=[BOsb])
            P.op(POOL, lambda e: e.tensor_scalar(out=Osb[:, 512:1024], in0=Osb[:, 512:1024], scalar1=neglam, scalar2=None, op0=ALU.mult),
                 reads=[BOsb], writes=[BOsb])
            P.op(POOL, lambda e: e.tensor_tensor(out=ofin, in0=Osb[:, 0:512], in1=Osb[:, 512:1024], op=ALU.add), reads=[BOsb], writes=[Bofin])
            P.op(POOL, lambda e: e.tensor_tensor(out=osq, in0=ofin, in1=ofin, op=ALU.mult), reads=[Bofin], writes=[Bosq])

            def part_b(h=h, q0=q0):
                sl = take_slot()
                P.op(PE, lambda e: e.matmul(Sslot(sl)[:, 0:512], lhsT=ones, rhs=osq, start=True, stop=True), reads=[Bosq], writes=[BS[sl]])
                P.op(ACT, lambda e: e.activation(out=rsb, in_=Sslot(sl)[:, 0:512], func=AF.Ln, scale=1.0 / 128, bias=epsb),
                     reads=[BS[sl]], writes=[Brsb])
                P.op(ACT, lambda e: e.activation(out=rsb, in_=rsb, func=AF.Exp, scale=-0.5), reads=[Brsb], writes=[Brsb])
                P.op(POOL, lambda e: e.tensor_scalar(out=ofin, in0=ofin, scalar1=sublnc, scalar2=None, op0=ALU.mult), reads=[Bofin], writes=[Bofin])
                P.op(POOL, lambda e: e.tensor_tensor(out=oT[:, h, q0:q0 + 512], in0=ofin, in1=rsb, op=ALU.mult), reads=[Bofin, Brsb], writes=[BoT])
            deferred.append((t + 16, part_b))

        loaded = -1

        def ensure_loaded(upto):
            nonlocal loaded
            while loaded < min(upto, len(jobs) - 1):
                loaded += 1
                load_chunk(loaded)

        ensure_loaded(1)
        issue_qk(0)
        issue_exp(0)
        for t in range(nsteps + 1):
            if t + 1 < nsteps:
                issue_qk(t + 1)
                issue_exp(t + 1)
            u = t - 1
            if u < 0:
                continue
            ensure_loaded(steps[u][0] + NKB - 1)
            prev_last = u > 0 and last_step_of_pass[(steps[u - 1][1], steps[u - 1][2], steps[u - 1][3])] == u - 1
            if prev_last:
                issue_den(u - 1)
                finish_pass(steps[u - 1][1], steps[u - 1][2], steps[u - 1][3], u - 1)
            issue_pv(u)
            issue_preadd(u)
            if u > 0 and not prev_last:
                issue_den(u - 1)
            while deferred and deferred[0][0] <= u:
                deferred.pop(0)[1]()
        issue_den(nsteps - 1)
        finish_pass(steps[-1][1], steps[-1][2], steps[-1][3], nsteps - 1)
        while deferred:
            deferred.pop(0)[1]()

        P.barrier()
        if debug and "C" in os.environ.get("DBG", "BC"):
            P.dma(SP, dbg["ot"], oT.rearrange("p h t -> p (h t)"), reads=[BoT], writes=[Buf("dbgo")], key="dbg")
            P.barrier()
        if stop == 2:
            P.emit()
            return nc

        A.cur = X1
        Wg3 = A.alloc(8 * 3072, BF16).rearrange("p (c n) -> p c n", c=8)
        Wpd = A.alloc(4 * DM, BF16).rearrange("p (c n) -> p c n", c=4)
        Wpm = A.alloc(4 * DM, BF16).rearrange("p (c n) -> p c n", c=4)
        Wout = A.alloc(8 * DM, BF16).rearrange("p (c n) -> p c n", c=8)
        hT3b = [A.alloc(8 * 512, BF16).rearrange("p (c n) -> p c n", c=8) for _ in range(2)]
        oTg = A.alloc(8 * 512, BF16).rearrange("p (c n) -> p c n", c=8)
        mT = A.alloc(8 * 512, BF16).rearrange("p (c n) -> p c n", c=8)
        P3B_END = A.cur
        A.cur = CONST_END
        sg = [A.alloc(512, F32) for _ in range(2)]
        sgd = [A.alloc(512, F32) for _ in range(2)]
        sgm = [A.alloc(512, F32) for _ in range(2)]
        m1 = A.alloc(512, F32)
        m2 = A.alloc(512, F32)
        xt = [A.alloc(DM, F32) for _ in range(2)]
        yb = [A.alloc(DM, F32) for _ in range(2)]
        junk = A.alloc(DM, F32)
        sm3 = A.alloc(16, F32)
        assert A.cur <= X0, (A.cur, X0)
        BWs, BWp, BWg, BWo = Buf("w3s"), Buf("w3p"), Buf("w3g"), Buf("w3o")
        wg3_v = wg3.rearrange("(c p) n -> p c n", p=128)
        for c in range(8):
            P.dma(POOL, Wg3[:, c, 0:1024], wg3_v[:, c, 0:1024], writes=[BWs], key="w3s")
        P.dma(POOL, Wpd, wpd.rearrange("(c p) n -> p c n", p=128), writes=[BWp], key="w3p")
        P.dma(POOL, Wpm, wpm.rearrange("(c p) n -> p c n", p=128), writes=[BWp], key="w3p")
        for c in range(8):
            P.dma(POOL, Wg3[:, c, 1024:3072], wg3_v[:, c, 1024:3072], writes=[BWg], key="w3g")
        wout_v = wout.rearrange("(c p) n -> p c n", p=128)
        for c in range(0, 8, 2):
            P.dma(POOL, Wout[:, c:c + 2, :], wout_v[:, c:c + 2, :], writes=[BWo], key="w3o")
        BhT3b, BoTg, BmT = [Buf("hT3a"), Buf("hT3b")], Buf("oTg"), Buf("mT")
        Bsg = [Buf("sg0"), Buf("sg1")]
        Bsgd = [Buf("sgd0"), Buf("sgd1")]
        Bsgm = [Buf("sgm0"), Buf("sgm1")]
        Bm1, Bm2 = Buf("m1"), Buf("m2")
        Bxt = [Buf("xt0"), Buf("xt1")]
        Byb = [Buf("yb0"), Buf("yb1")]
        Bjunk, Bsm3 = Buf("junk"), Buf("sm3")
        By = Buf("y")
        hTd_v = hT_d.rearrange("p (c t) -> p c t", c=8)
        ti = 0
        P.dma(SP, hT3b[0], hTd_v[:, :, 0:512], reads=[BhTd], writes=[BhT3b[0]])
        for G in range(4):
            tg = slice(G * 512, (G + 1) * 512)
            hT3, BhT3 = hT3b[G % 2], BhT3b[G % 2]
            if G + 1 < 4:
                P.dma(SP, hT3b[(G + 1) % 2], hTd_v[:, :, (G + 1) * 512:(G + 2) * 512], reads=[BhTd], writes=[BhT3b[(G + 1) % 2]])
            for j in range(8):
                b = proj_fm(Wg3, 8, j * 128, hT3, BhT3, BWx=BWs)
                k = j % 2
                P.op(ACT, lambda e, b=b, k=k: e.activation(out=sg[k], in_=bank(b), func=AF.Silu), reads=[psb[b]], writes=[Bsg[k]])
                P.op(DVE, lambda e, j=j, k=k, tg=tg: e.tensor_tensor(out=oTg[:, j, :], in0=oT[:, j, tg], in1=sg[k], op=ALU.mult),
                     reads=[Bsg[k], BoT], writes=[BoTg])
            for n in range(8):
                k = n % 2
                bd = next_bank()
                for hh in range(4):
                    P.op(PE, lambda e, hh=hh, bd=bd, n=n: e.matmul(bank(bd), lhsT=Wpd[:, hh, n * 128:(n + 1) * 128], rhs=oTg[:, hh, :],
                                                                  start=(hh == 0), stop=(hh == 3)),
                         reads=[BWp, BoTg], writes=[psb[bd]] if hh in (0, 3) else [], signal=(hh == 3))
                bm = next_bank()
                for hh in range(4):
                    P.op(PE, lambda e, hh=hh, bm=bm, n=n: e.matmul(bank(bm), lhsT=Wpm[:, hh, n * 128:(n + 1) * 128], rhs=oTg[:, 4 + hh, :],
                                                                  start=(hh == 0), stop=(hh == 3)),
                         reads=[BWp, BoTg], writes=[psb[bm]] if hh in (0, 3) else [], signal=(hh == 3))
                bgd = proj_fm(Wg3, 8, 1024 + n * 128, hT3, BhT3, BWx=BWg)
                bgm = proj_fm(Wg3, 8, 2048 + n * 128, hT3, BhT3, BWx=BWg)
                P.op(ACT, lambda e, b=bgd, k=k: e.activation(out=sgd[k], in_=bank(b), func=AF.Sigmoid), reads=[psb[bgd]], writes=[Bsgd[k]])
                P.op(ACT, lambda e, b=bgm, k=k: e.activation(out=sgm[k], in_=bank(b), func=AF.Sigmoid), reads=[psb[bgm]], writes=[Bsgm[k]])
                P.op(DVE, lambda e, bd=bd, k=k: e.tensor_tensor(out=m1, in0=bank(bd), in1=sgd[k], op=ALU.mult), reads=[psb[bd], Bsgd[k]], writes=[Bm1])
                P.op(DVE, lambda e, bm=bm, k=k: e.tensor_tensor(out=m2, in0=bank(bm), in1=sgm[k], op=ALU.mult), reads=[psb[bm], Bsgm[k]], writes=[Bm2])
                P.op(POOL, lambda e, n=n: e.tensor_tensor(out=mT[:, n, :], in0=m1, in1=m2, op=ALU.add), reads=[Bm1, Bm2], writes=[BmT])
            for tt in range(4):
                k = ti % 2
                ti += 1
                row0 = G * 512 + tt * 128
                P.dma(SP, xt[k], xtok[row0:row0 + 128, :], writes=[Bxt[k]])
                for hf in range(2):
                    b = next_bank()
                    for n in range(8):
                        P.op(PE, lambda e, n=n, b=b, tt=tt, hf=hf: e.matmul(bank(b), lhsT=mT[:, n, tt * 128:(tt + 1) * 128], rhs=Wout[:, n, hf * 512:(hf + 1) * 512],
                                                                       start=(n == 0), stop=(n == 7)),
                             reads=[BWo, BmT], writes=[psb[b]] if n in (0, 7) else [], signal=(n == 7))
                    P.op(DVE, lambda e, b=b, k=k, hf=hf: e.tensor_tensor(out=yb[k][:, hf * 512:(hf + 1) * 512], in0=bank(b), in1=xt[k][:, hf * 512:(hf + 1) * 512], op=ALU.add),
                         reads=[psb[b], Bxt[k]], writes=[Byb[k]])
                P.op(DVE, lambda e, k=k: e.scalar_tensor_tensor(out=junk, in0=yb[k], scalar=1.0, in1=yb[k], op0=ALU.mult, op1=ALU.mult, accum_out=sm3[:, 0:1]),
                     reads=[Byb[k], Bsm3], writes=[Bjunk, Bsm3])
                P.op(ACT, lambda e: e.activation(out=sm3[:, 1:2], in_=sm3[:, 0:1], func=AF.Sqrt, scale=1.0 / DM, bias=epsb), reads=[Bsm3], writes=[Bsm3])
                P.op(DVE, lambda e: e.reciprocal(out=sm3[:, 2:3], in_=sm3[:, 1:2]), reads=[Bsm3], writes=[Bsm3])
                P.op(DVE, lambda e, k=k: e.scalar_tensor_tensor(out=yb[k], in0=yb[k], scalar=sm3[:, 2:3], in1=nfb, op0=ALU.mult, op1=ALU.mult),
                     reads=[Byb[k], Bsm3, Bc], writes=[Byb[k]])
                P.dma(SP, y[row0:row0 + 128, :], yb[k], reads=[Byb[k]], writes=[By], key="yout%d" % k)
        P.barrier()
        P.emit()
    return nc


def _rot_cols(w):
    k, n = w.shape
    w4 = w.reshape(k, n // 64, 2, 32)
    return np.ascontiguousarray(w4[:, :, ::-1, :]).reshape(k, n)


def _prep_inputs(x, positions, norm_in, w_in, diff_lambda_q1, diff_lambda_k1, diff_lambda_q2, diff_lambda_k2,
                 diff_subln, mla_q_norm, w_uq, mla_kv_norm, w_ukv, w_proj_diff, w_proj_mla, w_out, norm_final):
    f32 = np.float32
    x = np.asarray(x, f32)[0]
    positions = np.asarray(positions)[0].astype(np.int32)
    w_in = np.asarray(w_in, f32)[0]
    o = 0
    sl = {}
    for name, n in (("dq", 512), ("dk", 512), ("dv", 512), ("dgate", 512), ("cq", 384), ("ckv", 256), ("kr", 64),
                    ("mgate", 512), ("gd", 1024), ("gm", 1024)):
        sl[name] = w_in[:, o:o + n]
        o += n
    kr2 = np.concatenate([sl["kr"], sl["kr"]], 1)
    wfm = np.concatenate([sl["dq"], sl["dk"], kr2, sl["cq"], sl["ckv"]], 1)
    assert wfm.shape[1] == NFM
    wg3 = np.concatenate([sl["dgate"], sl["mgate"], sl["gd"], sl["gm"]], 1)
    uq = np.asarray(w_uq, f32)[0].reshape(384, 4, 192)
    uq_n = uq[:, :, :128].reshape(384, 512)
    uq_r = uq[:, :, 128:].reshape(384, 256)
    wuq = np.concatenate([uq_n, uq_r], 1)
    ukv = np.asarray(w_ukv, f32)[0].reshape(256, 4, 256)
    wukv = np.concatenate([ukv[:, :, :128].reshape(256, 512), ukv[:, :, 128:].reshape(256, 512)], 1)
    cst = np.zeros((128, 16), f32)
    cst[:, 0:8] = np.asarray(norm_in, f32)[0].reshape(8, 128).T
    cst[:, 8:11] = np.asarray(mla_q_norm, f32)[0].reshape(3, 128).T
    cst[:, 11:13] = np.asarray(mla_kv_norm, f32)[0].reshape(2, 128).T
    inv_freq = (np.float32(10000.0) ** (-np.arange(0, 64, 2, dtype=np.float32) / np.float32(64))).astype(f32)
    pidx = np.arange(128)
    cst[:, 13] = inv_freq[pidx % 32]
    cst[:, 14] = np.where((pidx % 64) < 32, -1.0, 1.0)
    cst[:, 15] = np.asarray(diff_subln, f32).reshape(128)
    perm = np.zeros((128, 128), f32)
    perm[np.where((pidx % 64) < 32, pidx + 32, pidx - 32), pidx] = 1.0
    lamv = np.concatenate([np.asarray(a, f32)[0] for a in (diff_lambda_q1, diff_lambda_k1, diff_lambda_q2, diff_lambda_k2)])[None]
    shared = {
        "wfm": np.ascontiguousarray(wfm), "wdv": np.ascontiguousarray(sl["dv"]), "wuq": np.ascontiguousarray(wuq),
        "wukv": np.ascontiguousarray(wukv), "wg3": np.ascontiguousarray(wg3),
        "wpd": np.ascontiguousarray(np.asarray(w_proj_diff, f32)[0]), "wpm": np.ascontiguousarray(np.asarray(w_proj_mla, f32)[0]),
        "wout": np.ascontiguousarray(np.asarray(w_out, f32)[0]), "cst": cst, "lamv": np.ascontiguousarray(lamv),
        "subln": np.asarray(diff_subln, f32).reshape(1, 128), "nfin": np.asarray(norm_final, f32).reshape(1, DM),
        "ident": np.eye(128, dtype=f32), "perm": perm,
    }
    xt = x.reshape(TPC, NCORES, 128, DM)
    pt = positions.reshape(TPC, NCORES, 128)
    kk = np.arange(128)[:, None]
    qq = np.arange(128)[None, :]
    in_maps = []
    for c in range(NCORES):
        xc = np.ascontiguousarray(xt[:, c]).reshape(NT, DM)
        mask = np.zeros((128, 8, 128), f32)
        for j in range(8):
            r = j if USE_CC else (c + j) % NCORES
            if r > c:
                mask[:, j, :] = NEG
            elif r == c:
                mask[:, j, :] = np.where(kk > qq, NEG, 0.0)
        d = dict(shared)
        ranks = [(c + j) % NCORES for j in range(NSLOT)]
        xg = np.stack([xt[:, r].reshape(4, 512, 8, 128) for r in ranks])
        d["xT"] = np.ascontiguousarray(xg.transpose(0, 1, 4, 3, 2)).reshape(NSLOT * 4, 128, 8 * 512)
        d["xtok"] = xc
        d["pos"] = np.ascontiguousarray(np.concatenate([pt[:, r].reshape(NT) for r in ranks])).reshape(1, NSLOT * NT)
        d["mask"] = mask.reshape(128, 1024)
        in_maps.append(d)
    return in_maps


_NC_CACHE = {}


def kernel(**inputs):
    in_maps = _prep_inputs(**inputs)
    if "nc" not in _NC_CACHE:
        _NC_CACHE["nc"] = build_program(debug=bool(os.environ.get("DBG")))
    res = run_bass_kernel_spmd(_NC_CACHE["nc"], in_maps, core_ids=list(range(NCORES)))
    out = np.empty((TPC, NCORES, 128, DM), np.float32)
    for c in range(NCORES):
        out[:, c] = np.asarray(res.results[c]["y"], np.float32).reshape(TPC, 128, DM)
    return out.reshape(1, SEQ, DM)
```
